# Optimizing a Trainium2 kernel written in Bass

```python
import jax, jax.numpy as jnp
from jax import lax
import numpy as np


D_MODEL = 1024
BATCH = 8
SEQ = 4096
DEPTH = 1

POOL_WIDTH = D_MODEL // 2
POOL_WINDOWS = (2, 4, 8, 16)
POOL_GROUPS = len(POOL_WINDOWS)
POOL_GROUP_DIM = POOL_WIDTH // POOL_GROUPS
HEAD_DIM = 64
ATTN_WIDTH = D_MODEL - POOL_WIDTH
ATTN_HEADS = ATTN_WIDTH // HEAD_DIM
MIX_WIDTH = POOL_WIDTH + ATTN_WIDTH
MOBA_BLOCK = 256
MOBA_TOPK = 3
Q_CHUNK = 128
ROPE_THETA = 10000.0
MEM_LEN = 256
XATTN_HEADS = 4
XATTN_HEAD_DIM = D_MODEL // XATTN_HEADS
D_FF = 2816
CONV_WIDTH = 3
EPS = 1e-6

kernel_name = 'hybrid_pool_moba_xattn_convffn'


def _rms_norm(x, g):
    xf = x.astype(jnp.float32)
    y = xf * lax.rsqrt(jnp.mean(xf * xf, axis=-1, keepdims=True) + EPS)
    return (y * g.astype(jnp.float32)).astype(x.dtype)


def _rope(x, pos):
    half = HEAD_DIM // 2
    inv_freq = ROPE_THETA ** (-jnp.arange(half, dtype=jnp.float32) / half)
    ang = pos.astype(jnp.float32)[:, None] * inv_freq[None, :]
    cos = jnp.cos(ang).astype(x.dtype)
    sin = jnp.sin(ang).astype(x.dtype)
    x1, x2 = x[..., :half], x[..., half:]
    return jnp.concatenate([x1 * cos - x2 * sin, x2 * cos + x1 * sin], axis=-1)


def _pool_mixer(u, pool_w, pool_scale):
    B, S, _ = u.shape
    uf = u.astype(jnp.float32)
    cs = jnp.cumsum(uf, axis=1)
    t = jnp.arange(S)
    outs = []
    for g, w in enumerate(POOL_WINDOWS):
        sl = slice(g * POOL_GROUP_DIM, (g + 1) * POOL_GROUP_DIM)
        c = cs[..., sl]
        c_prev = jnp.pad(c, ((0, 0), (w, 0), (0, 0)))[:, :S]
        cnt = jnp.minimum(t + 1, w).astype(jnp.float32)
        outs.append((c - c_prev) / cnt[None, :, None] - uf[..., sl])
    pooled = jnp.stack(outs, axis=2).astype(u.dtype)
    mixed = jnp.einsum('bsgc,gcd->bsgd', pooled, pool_w)
    return mixed.reshape(B, S, POOL_WIDTH) * pool_scale


def _moba_attention(q, k, v):
    B, H, S, Dh = q.shape
    n_blocks = max(-(-S // MOBA_BLOCK), MOBA_TOPK)
    pad = n_blocks * MOBA_BLOCK - S
    kb = jnp.pad(k, ((0, 0), (0, 0), (0, pad), (0, 0))).reshape(B, H, n_blocks, MOBA_BLOCK, Dh)
    vb = jnp.pad(v, ((0, 0), (0, 0), (0, pad), (0, 0))).reshape(B, H, n_blocks, MOBA_BLOCK, Dh)
    k_mean = jnp.mean(kb.astype(jnp.float32), axis=3).astype(k.dtype)
    n_chunks = S // Q_CHUNK
    qc = q.reshape(B, H, n_chunks, Q_CHUNK, Dh).transpose(0, 2, 1, 3, 4)
    scale = HEAD_DIM ** -0.5
    h_idx = jnp.arange(H)[:, None, None]
    blk_ids = jnp.arange(n_blocks)
    slot_ids = jnp.arange(MOBA_TOPK)
    n_sel = MOBA_TOPK * MOBA_BLOCK

    def chunk_fn(q_c, c, kb_b, vb_b, km_b):
        q0 = c * Q_CHUNK
        qblk = q0 // MOBA_BLOCK
        q_pos = q0 + jnp.arange(Q_CHUNK)
        gate = jnp.einsum('hqd,hnd->hqn', q_c, km_b, preferred_element_type=jnp.float32)
        gate = jnp.where(blk_ids[None, None, :] < qblk, gate, -jnp.inf)
        _, sel = lax.top_k(gate, MOBA_TOPK)
        k_sel = kb_b[h_idx, sel]
        v_sel = vb_b[h_idx, sel]
        s_sel = jnp.einsum('hqd,hqkjd->hqkj', q_c, k_sel, preferred_element_type=jnp.float32) * scale
        s_sel = jnp.where((slot_ids < qblk)[None, None, :, None], s_sel, -jnp.inf)
        k_own = lax.dynamic_index_in_dim(kb_b, qblk, axis=1, keepdims=False)
        v_own = lax.dynamic_index_in_dim(vb_b, qblk, axis=1, keepdims=False)
        s_own = jnp.einsum('hqd,hjd->hqj', q_c, k_own, preferred_element_type=jnp.float32) * scale
        k_pos = qblk * MOBA_BLOCK + jnp.arange(MOBA_BLOCK)
        s_own = jnp.where((k_pos[None, :] <= q_pos[:, None])[None], s_own, -jnp.inf)
        scores = jnp.concatenate([s_sel.reshape(H, Q_CHUNK, n_sel), s_own], axis=-1)
        p = jax.nn.softmax(scores, axis=-1).astype(v_sel.dtype)
        p_sel = p[..., :n_sel].reshape(H, Q_CHUNK, MOBA_TOPK, MOBA_BLOCK)
        p_own = p[..., n_sel:]
        return (jnp.einsum('hqkj,hqkjd->hqd', p_sel, v_sel)
                + jnp.einsum('hqj,hjd->hqd', p_own, v_own))

    def batch_fn(args):
        q_b, kb_b, vb_b, km_b = args
        return lax.map(lambda a: chunk_fn(a[0], a[1], kb_b, vb_b, km_b),
                       (q_b, jnp.arange(n_chunks)))

    out = lax.map(batch_fn, (qc, kb, vb, k_mean))
    return out.transpose(0, 2, 1, 3, 4).reshape(B, H, S, Dh)


def _cross_attention(h, m, w_xq, w_xkv, xq_norm_g, xk_norm_g, w_xo):
    B, S, _ = h.shape
    M = m.shape[1]
    q = (h @ w_xq).reshape(B, S, XATTN_HEADS, XATTN_HEAD_DIM)
    kv = (m @ w_xkv).reshape(B, M, 2, XATTN_HEADS, XATTN_HEAD_DIM)
    k, v = kv[:, :, 0], kv[:, :, 1]
    q = _rms_norm(q, xq_norm_g)
    k = _rms_norm(k, xk_norm_g)
    s = jnp.einsum('bshd,bmhd->bhsm', q, k, preferred_element_type=jnp.float32) * (XATTN_HEAD_DIM ** -0.5)
    p = jax.nn.softmax(s, axis=-1).astype(v.dtype)
    o = jnp.einsum('bhsm,bmhd->bshd', p, v).reshape(B, S, D_MODEL)
    return o @ w_xo


def _conv_ffn(h, w_up, conv_w, conv_b, w_down):
    up = h @ w_up
    C = up.shape[-1]
    rhs = conv_w.reshape(CONV_WIDTH, 1, C).astype(up.dtype)
    up = lax.conv_general_dilated(up, rhs, window_strides=(1,), padding=[(CONV_WIDTH - 1, 0)],
                                  dimension_numbers=('NWC', 'WIO', 'NWC'),
                                  feature_group_count=C) + conv_b
    gate, val = up[..., :D_FF], up[..., D_FF:]
    return (jax.nn.silu(gate) * val) @ w_down


def setup_inputs(seed: int = 0) -> dict:
    key = jax.random.key(seed)
    ks = jax.random.split(key, 24)
    f32 = jnp.float32

    def nrm(k, shape, scale):
        return jax.random.normal(k, shape, f32) * scale

    def gain(k, shape):
        return 1.0 + 0.1 * jax.random.normal(k, shape, f32)

    L = DEPTH
    return {
        'x': jax.random.normal(ks[0], (BATCH, SEQ, D_MODEL), f32),
        'mem': jax.random.normal(ks[1], (BATCH, MEM_LEN, D_MODEL), f32),
        'norm_mix_g': gain(ks[2], (L, D_MODEL)),
        'w_in': nrm(ks[3], (L, D_MODEL, POOL_WIDTH + 3 * ATTN_WIDTH), D_MODEL ** -0.5),
        'pool_w': nrm(ks[4], (L, POOL_GROUPS, POOL_GROUP_DIM, POOL_GROUP_DIM), POOL_GROUP_DIM ** -0.5),
        'pool_scale': gain(ks[5], (L, POOL_WIDTH)),
        'q_norm_g': gain(ks[6], (L, HEAD_DIM)),
        'k_norm_g': gain(ks[7], (L, HEAD_DIM)),
        'w_out': nrm(ks[8], (L, MIX_WIDTH, D_MODEL), MIX_WIDTH ** -0.5),
        'norm_xattn_g': gain(ks[9], (L, D_MODEL)),
        'norm_mem_g': gain(ks[10], (L, D_MODEL)),
        'w_xq': nrm(ks[11], (L, D_MODEL, D_MODEL), D_MODEL ** -0.5),
        'w_xkv': nrm(ks[12], (L, D_MODEL, 2 * D_MODEL), D_MODEL ** -0.5),
        'xq_norm_g': gain(ks[13], (L, XATTN_HEAD_DIM)),
        'xk_norm_g': gain(ks[14], (L, XATTN_HEAD_DIM)),
        'w_xo': nrm(ks[15], (L, D_MODEL, D_MODEL), D_MODEL ** -0.5),
        'norm_ffn_g': gain(ks[16], (L, D_MODEL)),
        'w_up': nrm(ks[17], (L, D_MODEL, 2 * D_FF), D_MODEL ** -0.5),
        'conv_w': nrm(ks[18], (L, CONV_WIDTH, 2 * D_FF), CONV_WIDTH ** -0.5),
        'conv_b': nrm(ks[19], (L, 2 * D_FF), 0.01),
        'w_down': nrm(ks[20], (L, D_FF, D_MODEL), D_FF ** -0.5),
    }


def reference(x, mem, norm_mix_g, w_in, pool_w, pool_scale, q_norm_g, k_norm_g, w_out,
              norm_xattn_g, norm_mem_g, w_xq, w_xkv, xq_norm_g, xk_norm_g, w_xo,
              norm_ffn_g, w_up, conv_w, conv_b, w_down):
    B, S, _ = x.shape
    pos = jnp.arange(S)
    for l in range(DEPTH):
        h = _rms_norm(x, norm_mix_g[l])
        proj = h @ w_in[l]
        u = proj[..., :POOL_WIDTH]
        qkv = proj[..., POOL_WIDTH:].reshape(B, S, 3, ATTN_HEADS, HEAD_DIM)
        q = qkv[:, :, 0].transpose(0, 2, 1, 3)
        k = qkv[:, :, 1].transpose(0, 2, 1, 3)
        v = qkv[:, :, 2].transpose(0, 2, 1, 3)
        q = _rope(_rms_norm(q, q_norm_g[l]), pos)
        k = _rope(_rms_norm(k, k_norm_g[l]), pos)
        attn = _moba_attention(q, k, v).transpose(0, 2, 1, 3).reshape(B, S, ATTN_WIDTH)
        pool = _pool_mixer(u, pool_w[l], pool_scale[l])
        x = x + jnp.concatenate([pool, attn], axis=-1) @ w_out[l]
        h = _rms_norm(x, norm_xattn_g[l])
        m = _rms_norm(mem, norm_mem_g[l])
        x = x + _cross_attention(h, m, w_xq[l], w_xkv[l], xq_norm_g[l], xk_norm_g[l], w_xo[l])
        h = _rms_norm(x, norm_ffn_g[l])
        x = x + _conv_ffn(h, w_up[l], conv_w[l], conv_b[l], w_down[l])
    return x
```

```python
from contextlib import ExitStack

import numpy as np

import concourse.bass as bass
import concourse.mybir as mybir
from concourse.bass_utils import run_bass_kernel_spmd

F32 = mybir.dt.float32
BF16 = mybir.dt.bfloat16
AF = mybir.ActivationFunctionType
ALU = mybir.AluOpType
AX = mybir.AxisListType

P = 128
T = 512
NT = 8
S = 4096
D = 1024
KC = 8
DFF = 2816
NEG = -30000.0
EPS = 1e-6
DEBUG = False

T_BG = 1
T_FEED = 1
STOP_AFTER = "C"

V_GMIX, V_GX, V_GMEM, V_GFFN, V_PSC, V_QG, V_KG, V_XQG, V_XKG, V_CW, V_CB, NV = 0, 8, 16, 24, 32, 36, 37, 38, 40, 42, 174, 218
C_ID, C_ONES, C_BONES, C_TRI, C_ROT, NCB = 0, 128, 256, 384, 512, 640


def _isz(dt):
    return 4 if dt == F32 else 2


class Tracker:
    def __init__(self, nc, es):
        self.nc = nc
        self.es = es
        self.engs = {"pe": nc.tensor, "act": nc.scalar, "dve": nc.vector, "pool": nc.gpsimd, "sp": nc.sync}
        self.sem = {}
        self.cnt = {}
        self.known = {k: {} for k in self.engs}
        for k in self.engs:
            self.newsem(k)
        self.recs = {}
        self.onchip = {}
        self.psum_last = {}
        self.psum_names = set()
        self.nwait = 0
        self.ninst = 0

    def newsem(self, key):
        self.sem[key] = self.es.enter_context(self.nc.semaphore("s_" + key))
        self.cnt[key] = 0

    def sb(self, name, shape, dt):
        t = self.nc.alloc_sbuf_tensor(name, list(shape), dt)
        row = 1
        for s_ in shape[1:]:
            row *= s_
        self.onchip[name] = row
        self.recs[name] = []
        return t

    def ps(self, name, ncols=512):
        t = self.nc.alloc_psum_tensor(name, [P, ncols], F32)
        self.onchip[name] = ncols
        self.recs[name] = []
        self.psum_names.add(name)
        return t

    def region(self, ap):
        name = ap.name
        if name not in self.onchip:
            return None
        isz = _isz(ap.dtype)
        rowb = self.onchip[name] * 4 if False else None
        aps = ap.ap
        pstep, pcount = aps[0]
        off = ap.offset
        if pstep > 0:
            row = pstep
        else:
            row = self.onchip[name]
        p0 = off // row
        f0 = off % row
        ext = 1
        for st, c in aps[1:]:
            ext += (c - 1) * abs(st)
        return (name, p0, p0 + pcount, f0 * isz, (f0 + ext) * isz)

    def _carve(self, reg):
        out = []
        for r in self.recs[reg[0]]:
            if not self._ov(r, reg):
                out.append(r)
                continue
            p0, p1, b0, b1 = reg[1], reg[2], reg[3], reg[4]

            def piece(pp0, pp1, bb0, bb1):
                if pp0 < pp1 and bb0 < bb1:
                    out.append({"p0": pp0, "p1": pp1, "b0": bb0, "b1": bb1, "w": r["w"], "r": dict(r["r"])})
            piece(r["p0"], r["p1"], r["b0"], min(b0, r["b1"]))
            piece(r["p0"], r["p1"], max(b1, r["b0"]), r["b1"])
            mb0, mb1 = max(b0, r["b0"]), min(b1, r["b1"])
            piece(r["p0"], min(p0, r["p1"]), mb0, mb1)
            piece(max(p1, r["p0"]), r["p1"], mb0, mb1)
            piece(max(p0, r["p0"]), min(p1, r["p1"]), mb0, mb1)
        self.recs[reg[0]] = out

    @staticmethod
    def _ov(r, reg):
        return r["p0"] < reg[2] and reg[1] < r["p1"] and r["b0"] < reg[4] and reg[3] < r["b1"]

    def _emit(self, eng, semkey, inc_amt, fn, reads, writes, inc=True, waiter=None):
        waiter = waiter or eng
        raw = set()
        war = set()
        rregs = [self.region(a) for a in reads]
        wregs = [self.region(a) for a in writes]
        for reg in rregs:
            if reg is None:
                continue
            for r in self.recs[reg[0]]:
                if self._ov(r, reg) and r["w"] is not None:
                    raw.add(r["w"])
        for reg in wregs:
            if reg is None:
                continue
            for r in self.recs[reg[0]]:
                if self._ov(r, reg):
                    if r["w"] is not None:
                        war.add(r["w"])
                    for k, v in r["r"].items():
                        war.add((k, v))
        need = {}
        pbanks = set()
        for reg in rregs + wregs:
            if reg is not None and reg[0] in self.psum_names:
                for bi in range(reg[3] // 2048, (reg[4] - 1) // 2048 + 1):
                    pbanks.add((reg[0], bi))
        for bn in pbanks:
            for k, v in self.psum_last.setdefault(bn, {}).items():
                if k != semkey:
                    war.add((k, v))
        for k, v in raw:
            if k == semkey and eng == "pe":
                continue
            need[k] = max(need.get(k, 0), v)
        for k, v in war:
            if k == semkey and eng == "pe":
                continue
            need[k] = max(need.get(k, 0), v)
        kn = self.known[waiter]
        e = self.engs[waiter]
        for k, v in need.items():
            if kn.get(k, 0) >= v:
                continue
            e.wait_ge(self.sem[k], v)
            kn[k] = v
            self.nwait += 1
        ins = fn()
        self.ninst += 1
        if inc:
            ins.then_inc(self.sem[semkey], inc_amt)
            self.cnt[semkey] += inc_amt
            myval = self.cnt[semkey]
        else:
            myval = self.cnt[semkey] + inc_amt
        for bn in pbanks:
            self.psum_last[bn][semkey] = myval
        for reg in rregs:
            if reg is None:
                continue
            self._carve(reg)
            for r in self.recs[reg[0]]:
                if self._ov(r, reg):
                    r["r"][semkey] = max(r["r"].get(semkey, 0), myval)
        for reg in wregs:
            if reg is None:
                continue
            self._carve(reg)
            keep = [r for r in self.recs[reg[0]] if not self._ov(r, reg)]
            keep.append({"p0": reg[1], "p1": reg[2], "b0": reg[3], "b1": reg[4], "w": (semkey, myval), "r": {}})
            self.recs[reg[0]] = keep
        return ins

    def op(self, eng, meth, out, *args, **kw):
        reads = [a for a in list(args) + list(kw.values()) if isinstance(a, bass.AP)]
        e = self.engs[eng]
        return self._emit(eng, eng, 1, lambda: getattr(e, meth)(out, *args, **kw), reads, [out])

    def mm(self, out, lhsT, rhs, start, stop, inc):
        e = self.engs["pe"]
        return self._emit("pe", "pe", 1, lambda: e.matmul(out, lhsT, rhs, start=start, stop=stop),
                          [lhsT, rhs], [out], inc=inc)

    def dma(self, q, semkey, out, in_):
        e = self.engs[q]
        return self._emit(q, semkey, 16, lambda: e.dma_start(out=out, in_=in_), [in_], [out], waiter=q)


def build_program():
    nc = bass.Bass("TRN2", target_bir_lowering=False)
    es = ExitStack()
    es.enter_context(nc.allow_low_precision("bf16 matmul operands with fp32 accumulation by design"))
    tr = Tracker(nc, es)

    def din(name, shape):
        return nc.dram_tensor(name, list(shape), F32, kind="ExternalInput").ap()

    xT = din("xT", [D, S]).rearrange("(c p) t -> p c t", p=P)
    memT = din("memT", [D, 256]).rearrange("(c p) t -> p c t", p=P)
    w_in = din("w_in", [D, 2048]).rearrange("(k p) n -> p k n", p=P)
    w_out = din("w_out", [D, D]).rearrange("(k p) n -> p k n", p=P)
    w_xq = din("w_xq", [D, D]).rearrange("(k p) n -> p k n", p=P)
    w_xkv = din("w_xkv", [D, 2048]).rearrange("(k p) n -> p k n", p=P)
    w_xo = din("w_xo", [D, D]).rearrange("(k p) n -> p k n", p=P)
    w_up = din("w_up", [D, 2 * DFF]).rearrange("(k p) n -> p k n", p=P)
    wd0 = din("wd0", [8, P, 12 * 128]).rearrange("o p (k n) -> o p k n", k=12)
    wd1 = din("wd1", [8, P, 10 * 128]).rearrange("o p (k n) -> o p k n", k=10)
    pool_w = din("pool_w", [4, P, P]).rearrange("g c d -> c g d")
    vecs_d = din("vecs", [P, NV])
    cosT_d = din("cosT", [P, S])
    sinT_d = din("sinT", [P, S])
    cst_d = din("cst", [P, NCB])
    oneh_d = din("oneh", [16, S])
    invc_d = din("invc", [P, 64])
    yT = nc.dram_tensor("yT", [D, S], F32, kind="ExternalOutput").ap().rearrange("(c p) t -> p c t", p=P)

    KT = tr.sb("KT", [P, 8, S], BF16)
    VA = tr.sb("VA", [P, 32, 8, 65], BF16)
    XK = tr.sb("XK", [P, 8, 256], BF16)
    XV = tr.sb("XV", [P, 2, D], BF16)
    WR = tr.sb("WR", [P, 4, 2048], BF16)
    COS = tr.sb("COS", [P, T], F32)
    SIN = tr.sb("SIN", [P, T], F32)
    XT = tr.sb("XT", [P, KC, T], F32)
    OST = tr.sb("OST", [P, 2, T], F32)
    HT = tr.sb("HT", [P, KC, T], BF16)
    BFA = tr.sb("BFA", [P, 8192], BF16)
    PT = tr.sb("PT", [P, 4, T], BF16)
    SCR = tr.sb("SCR", [P, 6144], F32)
    SQR = tr.sb("SQR", [P, 2, T], BF16)
    CB = tr.sb("CB", [P, NCB], BF16)
    VEC = tr.sb("VEC", [P, NV], F32)
    INVC = tr.sb("INVC", [P, 64], F32)
    PW = tr.sb("PW", [P, 4, P], BF16)
    KM = tr.sb("KM", [P, 8, 16], BF16)
    KMF = tr.sb("KMF", [P, 8, 2], F32)
    UH = tr.sb("UH", [P, 4, 16], F32)
    ZZ = tr.sb("ZZ", [P, 2, 44, 4], F32)
    WBT = tr.sb("WBT", [P, 4, 44, 2], F32)
    ONE32 = tr.sb("ONE32", [P, 64], F32)
    BB = tr.sb("BB", [P, 192], BF16)
    XSB = tr.sb("XSB", [P, 2, T], BF16)

    QTA = BFA[0:80, 0:4096].rearrange("p (h t) -> p h t", h=8)
    MIX = BFA[:, 4096:8192].rearrange("p (c t) -> p c t", c=8)
    XQ = MIX
    XO = BFA[:, 0:4096].rearrange("p (c t) -> p c t", c=8)
    ACTT = BFA[:].rearrange("p (c t) -> p c t", c=16)

    R_N = SCR[:, 0:512]
    RD = R_N
    o = 512
    T1s = [SCR[:, o + 1536 * j:o + 1536 * j + 512] for j in range(2)]
    T2s = [SCR[:, o + 1536 * j + 512:o + 1536 * j + 1024] for j in range(2)]
    R2s = [SCR[:, o + 1536 * j + 1024:o + 1536 * j + 1536] for j in range(2)]
    R2 = R2s[0]
    o2 = o + 3072
    UB = SCR[:, o2:o2 + 528]
    SA = SCR[:, o2 + 528:o2 + 1056]
    SBb = SCR[:, o2 + 1056:o2 + 1584]
    PLf = SCR[:, o2 + 1584:o2 + 1840]
    PL = PLf.bitcast(BF16)
    G_ = SCR[:, o2 + 1840:o2 + 1968]
    M8 = SCR[:, o2 + 1968:o2 + 2032]
    TMP16 = SCR[:, o2 + 2032:o2 + 2048]
    GALL = SCR[:, o2 + 2048:o2 + 2560].rearrange("p (s n) -> p s n", s=4)
    RBS = SCR[:, o2 + 2048:o2 + 2560]
    RDEN = SCR[:, o:o + 512]
    AB = [SCR[:, o + 512 * j:o + 512 * (j + 1)] for j in range(4)]
    FXA = SCR[:, o + 2048:o + 2048 + 24].rearrange("p (c t) -> p c t", t=2)
    FXB = SCR[:, o + 2080:o + 2080 + 24].rearrange("p (c t) -> p c t", t=2)
    FXT = SCR[:, o + 2112:o + 2112 + 24].rearrange("p (c t) -> p c t", t=2)
    MTF = SCR[:, 512:512 + 2048].rearrange("p (c t) -> p c t", c=8)
    XSTG = SCR[:, 512:512 + 2048].rearrange("p (c t) -> p c t", c=4)

    DD = [tr.ps(f"dd{j}", 1024) for j in range(4)]
    PSB = [DD[k // 2][:, (k % 2) * 512:(k % 2 + 1) * 512] for k in range(8)]
    PT2v = PT[:].rearrange("p (u s) t -> p u (s t)", u=2)

    dbg_outs = []

    def dbg(name, ap):
        if not DEBUG:
            return
        tr.newsem("dbg_" + name)
        shp = list(ap.shape)
        dt_ = nc.dram_tensor("dbg_" + name, shp, F32, kind="ExternalOutput").ap()
        tr.dma("pool", "dbg_" + name, dt_, ap)
        dbg_outs.append("dbg_" + name)

    IDENT = CB[:, C_ID:C_ID + 128]
    ONES = CB[:, C_ONES:C_ONES + 128]
    BONES = CB[:, C_BONES:C_BONES + 128]
    TRI = CB[:, C_TRI:C_TRI + 128]
    ROTM = CB[:, C_ROT:C_ROT + 128]

    for k in ["cos", "sin", "ost0", "ost1"] + [f"wr{i}" for i in range(4)] + [f"x{i}" for i in range(8)]:
        tr.newsem(k)

    for k in ["i_vec", "i_invc", "i_cb", "i_pw", "i_mtf"] + [f"i_oh{h}" for h in range(8)]:
        tr.newsem(k)
    tr.dma("sp", "i_vec", VEC[:], vecs_d)
    tr.dma("pool", "i_cb", CB[:], cst_d)
    tr.dma("sp", "i_mtf", MTF, memT)
    for c in range(KC):
        tr.dma("sp", f"x{c}", XT[:, c, :], xT[:, c, 0:T])
    tr.dma("sp", "cos", COS[:], cosT_d[:, 0:T])
    tr.dma("sp", "sin", SIN[:], sinT_d[:, 0:T])
    tr.dma("sp", "i_invc", INVC[:], invc_d)
    tr.dma("pool", "i_pw", PW[:], pool_w)
    for h in range(8):
        tr.dma("pool", f"i_oh{h}", KT[64:80, h, :], oneh_d)

    tr.op("dve", "memset", VA[:, :, :, 64:65].rearrange("p a b c -> p (a b) c"), 1.0)
    tr.op("dve", "memset", KM[:].rearrange("p a b -> p (a b)"), 0.0)
    tr.op("dve", "memset", UH[:].rearrange("p a b -> p (a b)"), 0.0)
    tr.op("dve", "memset", ZZ[:].rearrange("p a b c -> p (a b c)"), 0.0)
    WB = [WBT[:, j] for j in range(4)]
    cwv = VEC[:, V_CW:V_CW + 132].rearrange("p (c j) -> p c j", j=3)
    for j in range(3):
        for t_ in range(2):
            tr.op("dve", "tensor_copy", WB[j][:, :, t_], cwv[:, :, j])
    for t_ in range(2):
        tr.op("dve", "tensor_copy", WB[3][:, :, t_], VEC[:, V_CB:V_CB + 44])
    tr.op("dve", "memset", BB[:], 0.0)
    tr.op("dve", "memset", ONE32[:], 1.0)

    wsched = []

    def sched_tile_pass():
        lst = []
        for j in [4, 5, 2, 3, 6, 7, 0, 1]:
            lst.append((f"in{j}", w_in[:, :, 256 * j:256 * j + 256]))
        for j in range(4):
            lst.append((f"out{j}", w_out[:, :, 256 * j:256 * j + 256]))
        if STOP_AFTER in ("B", "C"):
            for j in range(4):
                lst.append((f"xq{j}", w_xq[:, :, 256 * j:256 * j + 256]))
            for j in range(4):
                lst.append((f"xo{j}", w_xo[:, :, 256 * j:256 * j + 256]))
        if STOP_AFTER == "C":
            def up_pair(j):
                return [(f"upg{j}", w_up[:, :, 256 * j:256 * j + 256]),
                        (f"upv{j}", w_up[:, :, DFF + 256 * j:DFF + 256 * j + 256])]

            def down(hf):
                wd = wd0 if hf == 0 else wd1
                return [(f"dn{hf}_{oc}", wd[oc]) for oc in range(8)]
            for j in range(6):
                lst += up_pair(j)
            for j in range(6, 8):
                lst += up_pair(j)
            lst += down(0)
            for j in range(8, 11):
                lst += up_pair(j)
            lst += down(1)
        return lst

    if STOP_AFTER in ("B", "C"):
        for j in range(8):
            wsched.append((f"xkv{j}", w_xkv[:, :, 256 * j:256 * j + 256]))
    for i in range(NT):
        wsched += sched_tile_pass()
    wstate = {"next_load": 0, "next_get": 0}

    def w_issue():
        n = wstate["next_load"]
        if n >= len(wsched):
            return
        tag, src = wsched[n]
        slot = n % 4
        kk, nn = src.shape[1], src.shape[2]
        dst = WR[:, slot, 0:kk * nn].rearrange("p (k n) -> p k n", k=kk)
        tr.dma("pool", f"wr{slot}", dst, src)
        wstate["next_load"] = n + 1

    def w_get(tag):
        n = wstate["next_get"]
        t_, src = wsched[n]
        assert t_ == tag, (t_, tag)
        while wstate["next_load"] < min(n + 4, len(wsched)):
            w_issue()
        wstate["next_get"] = n + 1
        slot = n % 4
        kk, nn = src.shape[1], src.shape[2]
        return WR[:, slot, 0:kk * nn].rearrange("p (k n) -> p k n", k=kk)

    def w_flush():
        while wstate["next_load"] < min(wstate["next_get"] + 4, len(wsched)):
            w_issue()

    pstate = {}
    ppools = {"g": [0, 1, 2, 3, 4, 5, 6], "n": [7], "x": [0, 1, 2, 3], "y": [4, 5, 6, 7], "bg": [5], "x6": [0, 1, 2, 3, 4, 5], "a1": [6]}

    def bank(pool="g"):
        lst = ppools[pool]
        k = pstate.get(pool, 0)
        pstate[pool] = k + 1
        return PSB[lst[k % len(lst)]]

    def vcol(c):
        return VEC[:, c:c + 1]

    EPSV = tr.sb("EPSV", [P, 1], F32)
    tr.op("dve", "memset", EPSV[:], EPS)
    VEPS = EPSV[:, 0:1]

    sq_state = {"n": 0}

    class RmsAcc:
        def __init__(self, nchunks, ncols, lhs_ones, pool):
            self.n = nchunks
            self.ncols = ncols
            self.lhs = lhs_ones
            self.acc = bank(pool)
            self.i = 0

        def feed(self, ch):
            sl = SQR[:, sq_state["n"] % 2, 0:self.ncols]
            sq_state["n"] += 1
            tr.op("act", "activation", sl, ch, AF.Square)
            tr.mm(self.acc[:, 0:self.ncols], self.lhs, sl, start=(self.i == 0), stop=(self.i == self.n - 1), inc=True)
            self.i += 1

        def finish(self, scale_div, dst_r):
            assert self.i == self.n
            tr.op("act", "activation", dst_r, self.acc[:, 0:self.ncols], AF.Ln, bias=VEPS, scale=1.0 / scale_div)
            tr.op("act", "activation", dst_r, dst_r, AF.Exp, scale=-0.5)

    def rms_sum(chunks, ncols, lhs_ones, scale_div, dst_r, pool="g"):
        acc = RmsAcc(len(chunks), ncols, lhs_ones, pool)
        for ch in chunks:
            acc.feed(ch)
        acc.finish(scale_div, dst_r)

    def xnorm_apply(src, gcol0, dst, ncols):
        chunks = src if isinstance(src, list) else [src[:, c, 0:ncols] for c in range(KC)]
        for c in range(KC):
            tr.op("dve", "scalar_tensor_tensor", dst[:, c, 0:ncols], chunks[c], vcol(gcol0 + c),
                  R_N[:, 0:ncols], op0=ALU.mult, op1=ALU.mult)

    def xnorm(src, gcol0, dst, ncols):
        chunks = src if isinstance(src, list) else [src[:, c, 0:ncols] for c in range(KC)]
        rms_sum(chunks, ncols, ONES, float(D), R_N[:, 0:ncols], pool="n")
        xnorm_apply(chunks, gcol0, dst, ncols)

    def proj_fm(wt, nchunks_out, rhs_of_k, ncols, nk=KC, col0=0, pool="g", flush=True):
        outs = []
        for oc in range(nchunks_out):
            b = bank(pool)
            for k in range(nk):
                tr.mm(b[:, 0:ncols], wt[:, k, col0 + oc * 128:col0 + (oc + 1) * 128], rhs_of_k(k),
                      start=(k == 0), stop=(k == nk - 1), inc=(k == nk - 1))
            outs.append(b[:, 0:ncols])
        if flush:
            w_flush()
        return outs

    if STOP_AFTER in ("B", "C"):
        xnorm(MTF, V_GMEM, HT, 256)
        for jj in range(4):
            wt = w_get(f"xkv{jj}")
            xs = proj_fm(wt, 2, lambda k: HT[:, k, 0:256], 256)
            rms_sum(xs, 256, ONES, 256.0, R2[:, 0:256])
            for dc in range(2):
                tr.op("dve", "scalar_tensor_tensor", XK[:, jj * 2 + dc, :], xs[dc], vcol(V_XKG + dc),
                      R2[:, 0:256], op0=ALU.mult, op1=ALU.mult)
        for jj in range(4):
            wt = w_get(f"xkv{4 + jj}")
            for mt in range(2):
                b = bank()
                for k in range(KC):
                    tr.mm(b[:, 0:256], HT[:, k, mt * 128:(mt + 1) * 128], wt[:, k, :], start=(k == 0), stop=(k == KC - 1),
                          inc=(k == KC - 1))
                tr.op("dve", "tensor_copy", XV[:, mt, jj * 256:(jj + 1) * 256], b[:, 0:256])

    ost_n = {"n": 0}
    for i in range(NT):
        t0 = i * T
        if i > 0 and STOP_AFTER == "C":
            xnorm([XT[:, c, :] for c in range(4)] + [XSTG[:, c, :] for c in range(4)], V_GMIX, HT, T)
            for c in range(4):
                tr.op("act", "activation", XT[:, 4 + c, :], XSTG[:, c, :], AF.Copy)
        else:
            xnorm(XT, V_GMIX, HT, T)

        def hrhs(k):
            return HT[:, k, :]

        def qk_a(xp, c, is_k, buf):
            gcol = vcol(V_KG if is_k else V_QG)
            sq = SQR[:, sq_state["n"] % 2, :]
            sq_state["n"] += 1
            xsb = XSB[:, buf, :]
            tr.op("act", "activation", sq, xp, AF.Square)
            tr.op("act", "activation", xsb, xp, AF.Copy, scale=gcol)
            ssb = bank("y")
            tr.mm(ssb[:, :], BONES, sq, start=True, stop=True, inc=True)
            rtb = bank("y")
            tr.mm(rtb[:, :], ROTM, xsb, start=True, stop=True, inc=True)
            return ssb, rtb

        def qk_b(xp, c, is_k, buf, ssb, rtb):
            gcol = vcol(V_KG if is_k else V_QG)
            R2_, T1, T2 = R2s[buf], T1s[buf], T2s[buf]
            tr.op("act", "activation", R2_, ssb[:, :], AF.Ln, bias=VEPS, scale=1.0 / 64.0)
            tr.op("act", "activation", R2_, R2_, AF.Exp, scale=-0.5)
            tr.op("dve", "scalar_tensor_tensor", T1, xp, gcol, COS[:], op0=ALU.mult, op1=ALU.mult)
            tr.op("dve", "tensor_tensor", T2, rtb[:, :], SIN[:], op=ALU.mult)
            tr.op("dve", "tensor_tensor", T1, T1, T2, op=ALU.add)
            for par in range(2):
                h = 2 * c + par
                if is_k:
                    d_ap = KT[0:64, h, t0:t0 + T]
                else:
                    d_ap = QTA[0:64, h, :]
                tr.op("dve", "tensor_tensor", d_ap, T1[par * 64:(par + 1) * 64, :],
                      R2_[par * 64:(par + 1) * 64, :], op=ALU.mult)
                if is_k:
                    junk = T2.bitcast(BF16)[0:64, 0:256]
                    for bl in range(2):
                        tr.op("act", "activation", junk, KT[0:64, h, t0 + 256 * bl:t0 + 256 * (bl + 1)], AF.Copy,
                              accum_out=KMF[0:64, h, bl:bl + 1])

        chunks = []
        done_a = []
        nb = 0
        for is_k, tags in ((True, ("in4", "in5")), (False, ("in2", "in3"))):
            for jj, tag in enumerate(tags):
                wt = w_get(tag)
                xs = proj_fm(wt, 2, hrhs, T, pool="x")
                prev = done_a
                done_a = []
                for cc in range(2):
                    if cc < len(prev):
                        qk_b(*prev[cc])
                    ch = (xs[cc], jj * 2 + cc, is_k, nb % 2)
                    nb += 1
                    done_a.append(ch + qk_a(*ch))
        for ch in done_a:
            qk_b(*ch)

        tr.op("act", "activation", KM[0:64, :, 2 * i:2 * i + 2], KMF[0:64, :, :], AF.Copy)
        for s in range(4):
            gb = bank()
            for h in range(8):
                tr.mm(gb[:, h * 16:(h + 1) * 16], QTA[0:64, h, s * 128:(s + 1) * 128], KM[0:64, h, :],
                      start=True, stop=True, inc=(h == 7))
            tr.op("act", "activation", GALL[:, s, :], gb[:, 0:128], AF.Copy)

        for jj, tag in enumerate(("in6", "in7")):
            wt = w_get(tag)
            for s in range(4):
                b = bank()
                for k in range(KC):
                    tr.mm(b[:, 0:256], HT[:, k, s * 128:(s + 1) * 128], wt[:, k, :], start=(k == 0), stop=(k == KC - 1),
                          inc=(k == KC - 1))
                if s == 3:
                    w_flush()
                tr.op("act", "activation", VA[:, 4 * i + s, 4 * jj:4 * jj + 4, 0:64],
                      b[:, 0:256].rearrange("p (h d) -> p h d", h=4), AF.Copy)

        for s in range(4):
            qblk = 2 * i + s // 2
            BBv = BB[:, 64:192].rearrange("p (h n) -> p h n", h=8)
            if qblk == 0:
                tr.op("dve", "memset", BB[:, 64:192], 0.0)
            else:
                Gm = GALL[:, s, :].rearrange("p (h n) -> p h n", h=8)
                GW = G_.rearrange("p (h n) -> p h n", h=8)
                GE = PLf[:, 0:128].rearrange("p (h n) -> p h n", h=8)
                mx = M8[:, 0:8]
                mxb = mx.unsqueeze(2).to_broadcast([P, 8, 16])
                if qblk < 16:
                    tr.op("dve", "memset", Gm[:, :, qblk:16], -1e30)
                src_ = Gm
                for rnd in range(2):
                    tr.op("dve", "tensor_reduce", mx, src_, axis=AX.X, op=ALU.max)
                    tr.op("dve", "tensor_tensor", GE, src_, mxb, op=ALU.is_ge)
                    tr.op("dve", "scalar_tensor_tensor", GW, GE, -1e30, src_, op0=ALU.mult, op1=ALU.add)
                    src_ = GW
                tr.op("dve", "tensor_reduce", mx, GW, axis=AX.X, op=ALU.max)
                tr.op("dve", "tensor_tensor", GW, Gm, mxb, op=ALU.is_lt)
                tr.op("dve", "tensor_scalar", BBv, GW, NEG, None, op0=ALU.mult)
                if qblk < 16:
                    tr.op("dve", "memset", BBv[:, :, qblk:16], 0.0)
            for hh in range(2):
                pb = bank()
                for hl in range(4):
                    h = 4 * hh + hl
                    tr.mm(pb[0:80, hl * 128:(hl + 1) * 128], BB[:, 16 * h:16 * h + 80], IDENT, start=True, stop=True,
                          inc=(hl == 3))
                tr.op("act", "activation", QTA[64:80, 4 * hh:4 * hh + 4, s * 128:(s + 1) * 128],
                      pb[64:80, :].rearrange("p (h q) -> p h q", h=4), AF.Copy)

        if i + 1 < NT:
            tr.dma("sp", "cos", COS[:], cosT_d[:, t0 + T:t0 + 2 * T])
            tr.dma("sp", "sin", SIN[:], sinT_d[:, t0 + T:t0 + 2 * T])

        ustate = {}

        def u_task(g, pb_):
            if g % 2 == 0:
                ustate["wt"] = w_get(f"in{g // 2}")
            wt = ustate["wt"]
            c0 = (g % 2) * 128
            for k in range(KC):
                tr.mm(pb_, wt[:, k, c0:c0 + 128], HT[:, k, :], start=(k == 0), stop=(k == KC - 1), inc=(k == KC - 1))
            if g % 2 == 1:
                w_flush()
            w = 2 ** (g + 1)
            tr.op("dve", "tensor_copy", UB[:, 0:16], UH[:, g, :])
            tr.op("dve", "tensor_copy", UB[:, 16:528], pb_)
            tr.op("dve", "tensor_copy", UH[:, g, :], UB[:, 512:528])
            src = UB
            bufs = [SA, SBb]
            sh = 1
            nbb = 0
            lo = 0
            while sh < w:
                dstb = bufs[nbb % 2]
                lo += sh
                tr.op("dve", "tensor_tensor", dstb[:, lo:528], src[:, lo:528], src[:, lo - sh:528 - sh], op=ALU.add)
                src = dstb
                nbb += 1
                sh *= 2
            tr.op("dve", "scalar_tensor_tensor", PL, src[:, 16:528], 1.0 / w, UB[:, 16:528],
                  op0=ALU.mult, op1=ALU.subtract)
            if i == 0:
                tr.op("dve", "tensor_tensor", TMP16, src[:, 16:32], INVC[:, g * 16:(g + 1) * 16], op=ALU.mult)
                tr.op("dve", "tensor_tensor", PL[:, 0:16], TMP16, UB[:, 16:32], op=ALU.subtract)

        def u_task2(g, pb_):
            tr.mm(pb_, PW[:, g, :], PL, start=True, stop=True, inc=True)
            tr.op("dve", "tensor_scalar", MIX[:, g, :], pb_, vcol(V_PSC + g), None, op0=ALU.mult)

        SD = [DD[0], DD[1], DD[2]]
        OBK = [PSB[6], PSB[7]]
        PT3 = [PT2v[:, 0, :], PT2v[:, 1, :], T1s[0].bitcast(BF16)]
        nkt = 4 * i + 4
        units = []
        for h in range(8):
            for kt in range(0, 4 * i, 2):
                units.append(("pair", h, kt))
            for j in range(4):
                units.append(("single", h, 4 * i + j))
        nreal = len(units)
        for g in range(4):
            p1 = (g + 1) * nreal // 5 + 2 * g
            units.insert(p1, ("bg1", g, 0))
            units.insert(p1 + 4, ("bg2", g, 0))
        LA = 2
        deferred = []

        def qk(m):
            kind, h, kt = units[m]
            sd = SD[m % 3]
            if kind == "bg1":
                u_task(h, sd[:, 0:T])
            elif kind == "bg2":
                u_task2(h, sd[:, 0:T])
            elif kind == "pair":
                for e in range(2):
                    tr.mm(sd[:, e * T:(e + 1) * T], KT[0:80, h, (kt + e) * 128:(kt + e + 1) * 128], QTA[0:80, h, :],
                          start=True, stop=True, inc=(e == 1))
                tr.op("act", "activation", PT3[m % 3], sd[:, :], AF.Exp, scale=0.125)
            else:
                j = kt - 4 * i
                q0 = 128 * j
                tr.mm(sd[:, q0:T], KT[0:80, h, kt * 128:(kt + 1) * 128], QTA[0:80, h, q0:T], start=True, stop=False,
                      inc=False)
                tr.mm(sd[:, q0:q0 + 128], IDENT, TRI, start=False, stop=True, inc=True)
                tr.op("act", "activation", PT3[m % 3][:, q0:T], sd[:, q0:T], AF.Exp, scale=0.125)

        def pv(m):
            kind, h, kt = units[m]
            if kind in ("bg1", "bg2"):
                return
            ob = OBK[h % 2]
            if kind == "pair":
                for e in range(2):
                    tr.mm(ob[0:65, :], VA[:, kt + e, h, :], PT3[m % 3][:, e * T:(e + 1) * T], start=(kt + e == 0),
                          stop=False, inc=(e == 1))
            else:
                j = kt - 4 * i
                q0 = 128 * j
                last = (kt == nkt - 1)
                tr.mm(ob[0:65, q0:T], VA[:, kt, h, :], PT3[m % 3][:, q0:T], start=(kt == 0), stop=last, inc=True)
                if last:
                    if h == 7:
                        tr.op("act", "activation", RD[64:65, :], ob[64:65, :], AF.Ln)
                        tr.op("act", "activation", RD[64:65, :], RD[64:65, :], AF.Exp, scale=-1.0)
                    else:
                        tr.op("dve", "reciprocal", RD[64:65, :], ob[64:65, :])

                    def fin(h=h, ob=ob):
                        rb = ob[64:128, :]
                        tr.mm(rb, ONE32[64:65, 0:64], RD[64:65, :], start=True, stop=True, inc=True)
                        tr.op("dve", "tensor_copy", RBS[0:64, :], rb)
                        par = h % 2
                        tr.op("dve", "tensor_tensor", MIX[par * 64:(par + 1) * 64, 4 + h // 2, :], ob[0:64, :],
                              RBS[0:64, :], op=ALU.mult)
                    deferred.append((m + 2, fin))

        for m in range(len(units) + LA):
            if m < len(units):
                qk(m)
            if m - LA >= 0:
                pv(m - LA)
            while deferred and deferred[0][0] <= m - LA:
                deferred.pop(0)[1]()
        while deferred:
            deferred.pop(0)[1]()

        if i == 0:
            dbg("mix", MIX)
            dbg("qta", QTA)
            dbg("kt", KT[0:80, :, 0:512])
        nacc = RmsAcc(KC, T, ONES, "n") if STOP_AFTER in ("B", "C") else None
        pend_feed = []
        for jj in range(4):
            wt = w_get(f"out{jj}")
            xs = proj_fm(wt, 2, lambda k: MIX[:, k, :], T)
            for oc_ in pend_feed:
                nacc.feed(XT[:, oc_, :])
            pend_feed = []
            for cc in range(2):
                oc = jj * 2 + cc
                tr.op("dve", "tensor_tensor", XT[:, oc, :], xs[cc], XT[:, oc, :], op=ALU.add)
                if nacc is not None:
                    pend_feed.append(oc)
        for oc_ in pend_feed:
            nacc.feed(XT[:, oc_, :])

        if STOP_AFTER in ("B", "C"):
            nacc.finish(float(D), R_N)
            xnorm_apply(XT, V_GX, HT, T)
            def finish_xq(hd, xs):
                rms_sum(xs, T, ONES, 256.0, R2s[hd % 2], pool="a1")
                for dc in range(2):
                    tr.op("dve", "scalar_tensor_tensor", XQ[:, hd * 2 + dc, :], xs[dc], vcol(V_XQG + dc), R2s[hd % 2],
                          op0=ALU.mult, op1=ALU.mult)

            prev = None
            for hd in range(4):
                wt = w_get(f"xq{hd}")
                xs = proj_fm(wt, 2, hrhs, T, pool="x6")
                if prev is not None:
                    finish_xq(*prev)
                prev = (hd, xs)
            finish_xq(*prev)

            def xsc(hd):
                pslots = []
                for mt in range(2):
                    sb_ = bank("x")
                    for dc in range(2):
                        tr.mm(sb_[:, :], XK[:, hd * 2 + dc, mt * 128:(mt + 1) * 128], XQ[:, hd * 2 + dc, :],
                              start=(dc == 0), stop=(dc == 1), inc=(dc == 1))
                    slot = (hd * 2 + mt) % 4
                    tr.op("act", "activation", PT[:, slot, :], sb_[:, :], AF.Exp, scale=1.0 / 16.0)
                    pslots.append(slot)
                return pslots

            def xpv(hd, pslots):
                db = bank("y")
                for mt in range(2):
                    tr.mm(db[:, :], ONES, PT[:, pslots[mt], :], start=(mt == 0), stop=(mt == 1), inc=(mt == 1))
                tr.op("act", "activation", RDEN, db[:, :], AF.Ln)
                tr.op("act", "activation", RDEN, RDEN, AF.Exp, scale=-1.0)
                for dch in range(2):
                    ob = bank("y")
                    for mt in range(2):
                        tr.mm(ob[:, :], XV[:, mt, (hd * 2 + dch) * 128:(hd * 2 + dch + 1) * 128], PT[:, pslots[mt], :],
                              start=(mt == 0), stop=(mt == 1), inc=(mt == 1))
                    tr.op("dve", "tensor_tensor", XO[:, hd * 2 + dch, :], ob[:, :], RDEN, op=ALU.mult)

            prev = None
            for hd in range(4):
                ps_ = xsc(hd)
                if prev is not None:
                    xpv(*prev)
                prev = (hd, ps_)
            xpv(*prev)
            nacc = RmsAcc(KC, T, ONES, "n") if STOP_AFTER == "C" else None
            pend_feed = []
            for jj in range(4):
                wt = w_get(f"xo{jj}")
                xs = proj_fm(wt, 2, lambda k: XO[:, k, :], T)
                for oc_ in pend_feed:
                    nacc.feed(XT[:, oc_, :])
                pend_feed = []
                for cc in range(2):
                    oc = jj * 2 + cc
                    tr.op("dve", "tensor_tensor", XT[:, oc, :], xs[cc], XT[:, oc, :], op=ALU.add)
                    if nacc is not None:
                        pend_feed.append(oc)
            for oc_ in pend_feed:
                nacc.feed(XT[:, oc_, :])

        if STOP_AFTER == "C":
            nacc.finish(float(D), R_N)
            xnorm_apply(XT, V_GFFN, HT, T)
            ffn_n = {"n": 0}

            ZR = ZZ[:, i % 2]
            ZW = ZZ[:, (i + 1) % 2]

            def pre_chunk(pp, ch):
                r = ffn_n["n"] % 4
                ffn_n["n"] += 1
                A = AB[r]
                w0, w1, w2 = (vcol(V_CW + ch * 3 + jx) for jx in range(3))
                tr.op("act", "activation", A, pp, AF.Identity, bias=vcol(V_CB + ch), scale=w2)
                tr.op("act", "activation", ZR[:, ch, 2:4], pp[:, 0:2], AF.Copy)
                tr.op("act", "activation", ZW[:, ch, 0:2], pp[:, T - 2:T], AF.Copy)
                tr.op("dve", "scalar_tensor_tensor", A[:, 2:T], pp[:, 1:T - 1], w1, A[:, 2:T], op0=ALU.mult, op1=ALU.add)
                tr.op("dve", "scalar_tensor_tensor", A[:, 2:T], pp[:, 0:T - 2], w0, A[:, 2:T], op0=ALU.mult, op1=ALU.add)
                return A

            def up_pair(j, slot0):
                wg = w_get(f"upg{j}")
                gs = proj_fm(wg, 2, hrhs, T)
                wv = w_get(f"upv{j}")
                vs = proj_fm(wv, 2, hrhs, T)
                for cc in range(2):
                    ch = 2 * j + cc
                    ag = pre_chunk(gs[cc], ch)
                    av = pre_chunk(vs[cc], 22 + ch)
                    tr.op("act", "activation", ag, ag, AF.Silu)
                    tr.op("pool", "tensor_tensor", ACTT[:, slot0 + cc, :], ag, av, op=ALU.mult)

            def fix_cols01(c0, c1, slot0):
                n = c1 - c0
                ps_ = []
                for base in (0, 22):
                    lo, hi = base + c0, base + c1
                    pa = (FXA if base == 0 else FXB)[:, 0:n, :]
                    tmp = FXT[:, 0:n, :]
                    tr.op("dve", "tensor_tensor", pa, ZR[:, lo:hi, 2:4], WB[2][:, lo:hi, :], op=ALU.mult)
                    tr.op("dve", "tensor_tensor", tmp, ZR[:, lo:hi, 1:3], WB[1][:, lo:hi, :], op=ALU.mult)
                    tr.op("dve", "tensor_tensor", pa, pa, tmp, op=ALU.add)
                    tr.op("dve", "tensor_tensor", tmp, ZR[:, lo:hi, 0:2], WB[0][:, lo:hi, :], op=ALU.mult)
                    tr.op("dve", "tensor_tensor", pa, pa, tmp, op=ALU.add)
                    tr.op("dve", "tensor_tensor", pa, pa, WB[3][:, lo:hi, :], op=ALU.add)
                    ps_.append(pa)
                tr.op("act", "activation", ps_[0], ps_[0], AF.Silu)
                tr.op("dve", "tensor_tensor", ACTT[:, slot0:slot0 + n, 0:2], ps_[0], ps_[1], op=ALU.mult)

            def down(hf, slots):
                nk = len(slots)
                for oc in range(8):
                    wt = w_get(f"dn{hf}_{oc}")
                    b = bank()
                    for kk in range(nk):
                        tr.mm(b[:, :], wt[:, kk, :], ACTT[:, slots[kk], :], start=(kk == 0), stop=(kk == nk - 1),
                              inc=(kk == nk - 1))
                    w_flush()
                    if hf == 0:
                        tr.op("dve", "tensor_tensor", XT[:, oc, :], b[:, :], XT[:, oc, :], op=ALU.add)
                    else:
                        emit_out(oc, b)

            def emit_out(oc, b):
                sl = ost_n["n"] % 2
                ost_n["n"] += 1
                tr.op("dve", "tensor_tensor", OST[:, sl, :], b[:, :], XT[:, oc, :], op=ALU.add)
                tr.dma("sp", f"ost{sl}", yT[:, oc, t0:t0 + T], OST[:, sl, :])
                if i + 1 < NT and oc < 4:
                    tr.dma("sp", f"x{oc}", XT[:, oc, :], xT[:, oc, t0 + T:t0 + 2 * T])

            for j in range(6):
                up_pair(j, 2 * j)
            fix_cols01(0, 12, 0)
            for j in range(6, 8):
                up_pair(j, 12 + 2 * (j - 6))
            fix_cols01(12, 16, 12)
            down(0, list(range(12)))
            for j in range(8, 11):
                up_pair(j, 2 * (j - 8))
            fix_cols01(16, 22, 0)
            if i + 1 < NT:
                for c in range(4):
                    tr.dma("sp", f"x{4 + c}", XSTG[:, c, :], xT[:, 4 + c, t0 + T:t0 + 2 * T])
            down(1, [12, 13, 14, 15, 0, 1, 2, 3, 4, 5])
        else:
            for oc in range(8):
                sl = ost_n["n"] % 2
                ost_n["n"] += 1
                tr.op("dve", "tensor_copy", OST[:, sl, :], XT[:, oc, :])
                tr.dma("sp", f"ost{sl}", yT[:, oc, t0:t0 + T], OST[:, sl, :])
                if i + 1 < NT:
                    tr.dma("sp", f"x{oc}", XT[:, oc, :], xT[:, oc, t0 + T:t0 + 2 * T])

    for k in ("ost0", "ost1"):
        nc.sync.wait_ge(tr.sem[k], tr.cnt[k])
    for k in dbg_outs:
        nc.gpsimd.wait_ge(tr.sem[k], tr.cnt[k])
    es.close()
    return nc, tr


def _host_consts():
    half = 32
    inv_freq = (np.float32(10000.0) ** (-np.arange(half, dtype=np.float32) / np.float32(half))).astype(np.float32)
    pos = np.arange(S, dtype=np.float32)
    ang = (pos[:, None] * inv_freq[None, :]).astype(np.float32)
    cos = np.cos(ang).astype(np.float32)
    sin = np.sin(ang).astype(np.float32)
    idx = (np.arange(P) % 64) % 32
    cosT = np.ascontiguousarray(cos[:, idx].T)
    sinT = np.ascontiguousarray(sin[:, idx].T)
    cst = np.zeros((P, NCB), np.float32)
    cst[:, C_ID:C_ID + 128] = np.eye(P, dtype=np.float32)
    cst[:, C_ONES:C_ONES + 128] = 1.0
    for b in range(2):
        cst[64 * b:64 * b + 64, C_BONES + 64 * b:C_BONES + 64 * b + 64] = 1.0
    kk = np.arange(P)[:, None]
    qq = np.arange(P)[None, :]
    cst[:, C_TRI:C_TRI + 128] = np.where(kk <= qq, 0.0, NEG)
    rot = np.zeros((P, P), np.float32)
    for d in range(P):
        if (d % 64) < 32:
            rot[d + 32, d] = -1.0
        else:
            rot[d - 32, d] = 1.0
    cst[:, C_ROT:C_ROT + 128] = rot
    oneh = np.zeros((16, S), np.float32)
    for n in range(16):
        oneh[n, 256 * n:256 * n + 256] = 1.0
    invc = np.zeros((P, 64), np.float32)
    for g in range(4):
        w = 2 ** (g + 1)
        invc[:, g * 16:(g + 1) * 16] = (1.0 / np.minimum(np.arange(16) + 1, w)).astype(np.float32)[None, :]
    return cosT, sinT, cst, oneh, invc


_CACHE = {}


def kernel(x, mem, norm_mix_g, w_in, pool_w, pool_scale, q_norm_g, k_norm_g, w_out,
           norm_xattn_g, norm_mem_g, w_xq, w_xkv, xq_norm_g, xk_norm_g, w_xo,
           norm_ffn_g, w_up, conv_w, conv_b, w_down):
    f = lambda a: np.ascontiguousarray(np.asarray(a, dtype=np.float32))
    x = f(x)
    mem = f(mem)
    B = x.shape[0]
    vecs = np.zeros((P, NV), np.float32)
    vecs[:, V_GMIX:V_GMIX + 8] = f(norm_mix_g)[0].reshape(8, P).T
    vecs[:, V_GX:V_GX + 8] = f(norm_xattn_g)[0].reshape(8, P).T
    vecs[:, V_GMEM:V_GMEM + 8] = f(norm_mem_g)[0].reshape(8, P).T
    vecs[:, V_GFFN:V_GFFN + 8] = f(norm_ffn_g)[0].reshape(8, P).T
    vecs[:, V_PSC:V_PSC + 4] = f(pool_scale)[0].reshape(4, P).T
    vecs[:, V_QG] = np.tile(f(q_norm_g)[0], 2)
    vecs[:, V_KG] = np.tile(f(k_norm_g)[0], 2)
    vecs[:, V_XQG:V_XQG + 2] = f(xq_norm_g)[0].reshape(2, P).T
    vecs[:, V_XKG:V_XKG + 2] = f(xk_norm_g)[0].reshape(2, P).T
    cw = f(conv_w)[0]
    vecs[:, V_CW:V_CW + 132] = cw.reshape(3, 44, P).transpose(2, 1, 0).reshape(P, 132)
    vecs[:, V_CB:V_CB + 44] = f(conv_b)[0].reshape(44, P).T
    cosT, sinT, cst, oneh, invc = _host_consts()
    wdn = f(w_down)[0]
    wd0 = np.ascontiguousarray(wdn[0:1536].reshape(12, P, 8, 128).transpose(2, 1, 0, 3).reshape(8, P, 12 * 128))
    wd1 = np.ascontiguousarray(wdn[1536:2816].reshape(10, P, 8, 128).transpose(2, 1, 0, 3).reshape(8, P, 10 * 128))
    if "nc" not in _CACHE:
        _CACHE["nc"] = build_program()[0]
    nc = _CACHE["nc"]
    shared = {
        "w_in": f(w_in)[0], "w_out": f(w_out)[0], "w_xq": f(w_xq)[0], "w_xkv": f(w_xkv)[0], "w_xo": f(w_xo)[0],
        "w_up": f(w_up)[0], "wd0": wd0, "wd1": wd1, "pool_w": f(pool_w)[0], "vecs": vecs, "cosT": cosT, "sinT": sinT,
        "cst": cst, "oneh": oneh, "invc": invc,
    }
    in_maps = []
    for b in range(B):
        m = dict(shared)
        m["xT"] = np.ascontiguousarray(x[b].T)
        m["memT"] = np.ascontiguousarray(mem[b].T)
        in_maps.append(m)
    res = run_bass_kernel_spmd(nc, in_maps, core_ids=list(range(B)))
    out = np.stack([np.ascontiguousarray(res.results[b]["yT"].T) for b in range(B)], axis=0)
    return out.astype(np.float32)
```

```python
from contextlib import ExitStack

import numpy as np

import concourse.bass as bass
import concourse.mybir as mybir
from concourse.bass_utils import run_bass_kernel_spmd

F32 = mybir.dt.float32
BF16 = mybir.dt.bfloat16
AF = mybir.ActivationFunctionType
ALU = mybir.AluOpType
AX = mybir.AxisListType

P = 128
T = 512
NT = 8
S = 4096
D = 1024
KC = 8
DFF = 2816
NEG = -30000.0
EPS = 1e-6
DEBUG = False

T_BG = 1
T_FEED = 1
STOP_AFTER = "C"

V_GMIX, V_GX, V_GMEM, V_GFFN, V_PSC, V_QG, V_KG, V_XQG, V_XKG, V_CW, V_CB, NV = 0, 8, 16, 24, 32, 36, 37, 38, 40, 42, 174, 218
C_ID, C_ONES, C_BONES, C_TRI, C_ROT, NCB = 0, 128, 256, 384, 512, 640


def _isz(dt):
    return 4 if dt == F32 else 2


class Tracker:
    def __init__(self, nc, es):
        self.nc = nc
        self.es = es
        self.engs = {"pe": nc.tensor, "act": nc.scalar, "dve": nc.vector, "pool": nc.gpsimd, "sp": nc.sync}
        self.sem = {}
        self.cnt = {}
        self.known = {k: {} for k in self.engs}
        for k in self.engs:
            self.newsem(k)
        self.recs = {}
        self.onchip = {}
        self.psum_last = {}
        self.psum_names = set()
        self.nwait = 0
        self.ninst = 0

    def newsem(self, key):
        self.sem[key] = self.es.enter_context(self.nc.semaphore("s_" + key))
        self.cnt[key] = 0

    def sb(self, name, shape, dt):
        t = self.nc.alloc_sbuf_tensor(name, list(shape), dt)
        row = 1
        for s_ in shape[1:]:
            row *= s_
        self.onchip[name] = row
        self.recs[name] = []
        return t

    def ps(self, name, ncols=512):
        t = self.nc.alloc_psum_tensor(name, [P, ncols], F32)
        self.onchip[name] = ncols
        self.recs[name] = []
        self.psum_names.add(name)
        return t

    def region(self, ap):
        name = ap.name
        if name not in self.onchip:
            return None
        isz = _isz(ap.dtype)
        rowb = self.onchip[name] * 4 if False else None
        aps = ap.ap
        pstep, pcount = aps[0]
        off = ap.offset
        if pstep > 0:
            row = pstep
        else:
            row = self.onchip[name]
        p0 = off // row
        f0 = off % row
        ext = 1
        for st, c in aps[1:]:
            ext += (c - 1) * abs(st)
        return (name, p0, p0 + pcount, f0 * isz, (f0 + ext) * isz)

    def _carve(self, reg):
        out = []
        for r in self.recs[reg[0]]:
            if not self._ov(r, reg):
                out.append(r)
                continue
            p0, p1, b0, b1 = reg[1], reg[2], reg[3], reg[4]

            def piece(pp0, pp1, bb0, bb1):
                if pp0 < pp1 and bb0 < bb1:
                    out.append({"p0": pp0, "p1": pp1, "b0": bb0, "b1": bb1, "w": r["w"], "r": dict(r["r"])})
            piece(r["p0"], r["p1"], r["b0"], min(b0, r["b1"]))
            piece(r["p0"], r["p1"], max(b1, r["b0"]), r["b1"])
            mb0, mb1 = max(b0, r["b0"]), min(b1, r["b1"])
            piece(r["p0"], min(p0, r["p1"]), mb0, mb1)
            piece(max(p1, r["p0"]), r["p1"], mb0, mb1)
            piece(max(p0, r["p0"]), min(p1, r["p1"]), mb0, mb1)
        self.recs[reg[0]] = out

    @staticmethod
    def _ov(r, reg):
        return r["p0"] < reg[2] and reg[1] < r["p1"] and r["b0"] < reg[4] and reg[3] < r["b1"]

    def _emit(self, eng, semkey, inc_amt, fn, reads, writes, inc=True, waiter=None):
        waiter = waiter or eng
        raw = set()
        war = set()
        rregs = [self.region(a) for a in reads]
        wregs = [self.region(a) for a in writes]
        for reg in rregs:
            if reg is None:
                continue
            for r in self.recs[reg[0]]:
                if self._ov(r, reg) and r["w"] is not None:
                    raw.add(r["w"])
        for reg in wregs:
            if reg is None:
                continue
            for r in self.recs[reg[0]]:
                if self._ov(r, reg):
                    if r["w"] is not None:
                        war.add(r["w"])
                    for k, v in r["r"].items():
                        war.add((k, v))
        need = {}
        pbanks = set()
        for reg in rregs + wregs:
            if reg is not None and reg[0] in self.psum_names:
                for bi in range(reg[3] // 2048, (reg[4] - 1) // 2048 + 1):
                    pbanks.add((reg[0], bi))
        for bn in pbanks:
            for k, v in self.psum_last.setdefault(bn, {}).items():
                if k != semkey:
                    war.add((k, v))
        for k, v in raw:
            if k == semkey and eng == "pe":
                continue
            need[k] = max(need.get(k, 0), v)
        for k, v in war:
            if k == semkey and eng == "pe":
                continue
            need[k] = max(need.get(k, 0), v)
        kn = self.known[waiter]
        e = self.engs[waiter]
        for k, v in need.items():
            if kn.get(k, 0) >= v:
                continue
            e.wait_ge(self.sem[k], v)
            kn[k] = v
            self.nwait += 1
        ins = fn()
        self.ninst += 1
        if inc:
            ins.then_inc(self.sem[semkey], inc_amt)
            self.cnt[semkey] += inc_amt
            myval = self.cnt[semkey]
        else:
            myval = self.cnt[semkey] + inc_amt
        for bn in pbanks:
            self.psum_last[bn][semkey] = myval
        for reg in rregs:
            if reg is None:
                continue
            self._carve(reg)
            for r in self.recs[reg[0]]:
                if self._ov(r, reg):
                    r["r"][semkey] = max(r["r"].get(semkey, 0), myval)
        for reg in wregs:
            if reg is None:
                continue
            self._carve(reg)
            keep = [r for r in self.recs[reg[0]] if not self._ov(r, reg)]
            keep.append({"p0": reg[1], "p1": reg[2], "b0": reg[3], "b1": reg[4], "w": (semkey, myval), "r": {}})
            self.recs[reg[0]] = keep
        return ins

    def op(self, eng, meth, out, *args, **kw):
        reads = [a for a in list(args) + list(kw.values()) if isinstance(a, bass.AP)]
        e = self.engs[eng]
        return self._emit(eng, eng, 1, lambda: getattr(e, meth)(out, *args, **kw), reads, [out])

    def mm(self, out, lhsT, rhs, start, stop, inc):
        e = self.engs["pe"]
        return self._emit("pe", "pe", 1, lambda: e.matmul(out, lhsT, rhs, start=start, stop=stop),
                          [lhsT, rhs], [out], inc=inc)

    def dma(self, q, semkey, out, in_):
        e = self.engs[q]
        return self._emit(q, semkey, 16, lambda: e.dma_start(out=out, in_=in_), [in_], [out], waiter=q)


def build_program():
    nc = bass.Bass("TRN2", target_bir_lowering=False)
    es = ExitStack()
    es.enter_context(nc.allow_low_precision("bf16 matmul operands with fp32 accumulation by design"))
    tr = Tracker(nc, es)

    def din(name, shape):
        return nc.dram_tensor(name, list(shape), F32, kind="ExternalInput").ap()

    xT = din("xT", [D, S]).rearrange("(c p) t -> p c t", p=P)
    memT = din("memT", [D, 256]).rearrange("(c p) t -> p c t", p=P)
    w_in = din("w_in", [D, 2048]).rearrange("(k p) n -> p k n", p=P)
    w_out = din("w_out", [D, D]).rearrange("(k p) n -> p k n", p=P)
    w_xq = din("w_xq", [D, D]).rearrange("(k p) n -> p k n", p=P)
    w_xkv = din("w_xkv", [D, 2048]).rearrange("(k p) n -> p k n", p=P)
    w_xo = din("w_xo", [D, D]).rearrange("(k p) n -> p k n", p=P)
    w_up = din("w_up", [D, 2 * DFF]).rearrange("(k p) n -> p k n", p=P)
    wd0 = din("wd0", [8, P, 12 * 128]).rearrange("o p (k n) -> o p k n", k=12)
    wd1 = din("wd1", [8, P, 10 * 128]).rearrange("o p (k n) -> o p k n", k=10)
    pool_w = din("pool_w", [4, P, P]).rearrange("g c d -> c g d")
    vecs_d = din("vecs", [P, NV])
    cosT_d = din("cosT", [P, S])
    sinT_d = din("sinT", [P, S])
    cst_d = din("cst", [P, NCB])
    oneh_d = din("oneh", [16, S])
    invc_d = din("invc", [P, 64])
    yT = nc.dram_tensor("yT", [D, S], F32, kind="ExternalOutput").ap().rearrange("(c p) t -> p c t", p=P)

    KT = tr.sb("KT", [P, 8, S], BF16)
    VA = tr.sb("VA", [P, 32, 8, 65], BF16)
    XK = tr.sb("XK", [P, 8, 256], BF16)
    XV = tr.sb("XV", [P, 2, D], BF16)
    WR = tr.sb("WR", [P, 4, 2048], BF16)
    COS = tr.sb("COS", [P, T], F32)
    SIN = tr.sb("SIN", [P, T], F32)
    XT = tr.sb("XT", [P, KC, T], F32)
    OST = tr.sb("OST", [P, 2, T], F32)
    HT = tr.sb("HT", [P, KC, T], BF16)
    BFA = tr.sb("BFA", [P, 8192], BF16)
    PT = tr.sb("PT", [P, 4, T], BF16)
    SCR = tr.sb("SCR", [P, 6144], F32)
    SQR = tr.sb("SQR", [P, 2, T], BF16)
    CB = tr.sb("CB", [P, NCB], BF16)
    VEC = tr.sb("VEC", [P, NV], F32)
    INVC = tr.sb("INVC", [P, 64], F32)
    PW = tr.sb("PW", [P, 4, P], BF16)
    KM = tr.sb("KM", [P, 8, 16], BF16)
    KMF = tr.sb("KMF", [P, 8, 2], F32)
    UH = tr.sb("UH", [P, 4, 16], F32)
    ZZ = tr.sb("ZZ", [P, 2, 44, 4], F32)
    WBT = tr.sb("WBT", [P, 4, 44, 2], F32)
    ONE32 = tr.sb("ONE32", [P, 64], F32)
    BB = tr.sb("BB", [P, 192], BF16)
    XSB = tr.sb("XSB", [P, 2, T], BF16)

    QTA = BFA[0:80, 0:4096].rearrange("p (h t) -> p h t", h=8)
    MIX = BFA[:, 4096:8192].rearrange("p (c t) -> p c t", c=8)
    XQ = MIX
    XO = BFA[:, 0:4096].rearrange("p (c t) -> p c t", c=8)
    ACTT = BFA[:].rearrange("p (c t) -> p c t", c=16)

    R_N = SCR[:, 0:512]
    RD = R_N
    o = 512
    T1s = [SCR[:, o + 1536 * j:o + 1536 * j + 512] for j in range(2)]
    T2s = [SCR[:, o + 1536 * j + 512:o + 1536 * j + 1024] for j in range(2)]
    R2s = [SCR[:, o + 1536 * j + 1024:o + 1536 * j + 1536] for j in range(2)]
    R2 = R2s[0]
    o2 = o + 3072
    UB = SCR[:, o2:o2 + 528]
    SA = SCR[:, o2 + 528:o2 + 1056]
    SBb = SCR[:, o2 + 1056:o2 + 1584]
    PLf = SCR[:, o2 + 1584:o2 + 1840]
    PL = PLf.bitcast(BF16)
    G_ = SCR[:, o2 + 1840:o2 + 1968]
    M8 = SCR[:, o2 + 1968:o2 + 2032]
    TMP16 = SCR[:, o2 + 2032:o2 + 2048]
    GALL = SCR[:, o2 + 2048:o2 + 2560].rearrange("p (s n) -> p s n", s=4)
    RBS = SCR[:, o2 + 2048:o2 + 2560]
    RDEN = SCR[:, o:o + 512]
    AB = [SCR[:, o + 512 * j:o + 512 * (j + 1)] for j in range(4)]
    FXA = SCR[:, o + 2048:o + 2048 + 24].rearrange("p (c t) -> p c t", t=2)
    FXB = SCR[:, o + 2080:o + 2080 + 24].rearrange("p (c t) -> p c t", t=2)
    FXT = SCR[:, o + 2112:o + 2112 + 24].rearrange("p (c t) -> p c t", t=2)
    MTF = SCR[:, 512:512 + 2048].rearrange("p (c t) -> p c t", c=8)
    XSTG = SCR[:, 512:512 + 2048].rearrange("p (c t) -> p c t", c=4)

    DD = [tr.ps(f"dd{j}", 1024) for j in range(4)]
    PSB = [DD[k // 2][:, (k % 2) * 512:(k % 2 + 1) * 512] for k in range(8)]
    PT2v = PT[:].rearrange("p (u s) t -> p u (s t)", u=2)

    dbg_outs = []

    def dbg(name, ap):
        if not DEBUG:
            return
        tr.newsem("dbg_" + name)
        shp = list(ap.shape)
        dt_ = nc.dram_tensor("dbg_" + name, shp, F32, kind="ExternalOutput").ap()
        tr.dma("pool", "dbg_" + name, dt_, ap)
        dbg_outs.append("dbg_" + name)

    IDENT = CB[:, C_ID:C_ID + 128]
    ONES = CB[:, C_ONES:C_ONES + 128]
    BONES = CB[:, C_BONES:C_BONES + 128]
    TRI = CB[:, C_TRI:C_TRI + 128]
    ROTM = CB[:, C_ROT:C_ROT + 128]

    for k in ["cos", "sin", "ost0", "ost1"] + [f"wr{i}" for i in range(4)] + [f"x{i}" for i in range(8)]:
        tr.newsem(k)

    for k in ["i_vec", "i_invc", "i_cb", "i_pw", "i_mtf"] + [f"i_oh{h}" for h in range(8)]:
        tr.newsem(k)
    tr.dma("sp", "i_vec", VEC[:], vecs_d)
    tr.dma("pool", "i_cb", CB[:], cst_d)
    tr.dma("sp", "i_mtf", MTF, memT)
    for c in range(KC):
        tr.dma("sp", f"x{c}", XT[:, c, :], xT[:, c, 0:T])
    tr.dma("sp", "cos", COS[:], cosT_d[:, 0:T])
    tr.dma("sp", "sin", SIN[:], sinT_d[:, 0:T])
    tr.dma("sp", "i_invc", INVC[:], invc_d)
    tr.dma("pool", "i_pw", PW[:], pool_w)
    for h in range(8):
        tr.dma("pool", f"i_oh{h}", KT[64:80, h, :], oneh_d)

    tr.op("dve", "memset", VA[:, :, :, 64:65].rearrange("p a b c -> p (a b) c"), 1.0)
    tr.op("dve", "memset", KM[:].rearrange("p a b -> p (a b)"), 0.0)
    tr.op("dve", "memset", UH[:].rearrange("p a b -> p (a b)"), 0.0)
    tr.op("dve", "memset", ZZ[:].rearrange("p a b c -> p (a b c)"), 0.0)
    WB = [WBT[:, j] for j in range(4)]
    cwv = VEC[:, V_CW:V_CW + 132].rearrange("p (c j) -> p c j", j=3)
    for j in range(3):
        for t_ in range(2):
            tr.op("dve", "tensor_copy", WB[j][:, :, t_], cwv[:, :, j])
    for t_ in range(2):
        tr.op("dve", "tensor_copy", WB[3][:, :, t_], VEC[:, V_CB:V_CB + 44])
    tr.op("dve", "memset", BB[:], 0.0)
    tr.op("dve", "memset", ONE32[:], 1.0)

    wsched = []

    def sched_tile_pass():
        lst = []
        for j in [4, 5, 2, 3, 6, 7, 0, 1]:
            lst.append((f"in{j}", w_in[:, :, 256 * j:256 * j + 256]))
        for j in range(4):
            lst.append((f"out{j}", w_out[:, :, 256 * j:256 * j + 256]))
        if STOP_AFTER in ("B", "C"):
            for j in range(4):
                lst.append((f"xq{j}", w_xq[:, :, 256 * j:256 * j + 256]))
            for j in range(4):
                lst.append((f"xo{j}", w_xo[:, :, 256 * j:256 * j + 256]))
        if STOP_AFTER == "C":
            def up_pair(j):
                return [(f"upg{j}", w_up[:, :, 256 * j:256 * j + 256]),
                        (f"upv{j}", w_up[:, :, DFF + 256 * j:DFF + 256 * j + 256])]

            def down(hf):
                wd = wd0 if hf == 0 else wd1
                return [(f"dn{hf}_{oc}", wd[oc]) for oc in range(8)]
            for j in range(6):
                lst += up_pair(j)
            for j in range(6, 8):
                lst += up_pair(j)
            lst += down(0)
            for j in range(8, 11):
                lst += up_pair(j)
            lst += down(1)
        return lst

    if STOP_AFTER in ("B", "C"):
        for j in range(8):
            wsched.append((f"xkv{j}", w_xkv[:, :, 256 * j:256 * j + 256]))
    for i in range(NT):
        wsched += sched_tile_pass()
    wstate = {"next_load": 0, "next_get": 0}

    def w_issue():
        n = wstate["next_load"]
        if n >= len(wsched):
            return
        tag, src = wsched[n]
        slot = n % 4
        kk, nn = src.shape[1], src.shape[2]
        dst = WR[:, slot, 0:kk * nn].rearrange("p (k n) -> p k n", k=kk)
        tr.dma("pool", f"wr{slot}", dst, src)
        wstate["next_load"] = n + 1

    def w_get(tag):
        n = wstate["next_get"]
        t_, src = wsched[n]
        assert t_ == tag, (t_, tag)
        while wstate["next_load"] < min(n + 4, len(wsched)):
            w_issue()
        wstate["next_get"] = n + 1
        slot = n % 4
        kk, nn = src.shape[1], src.shape[2]
        return WR[:, slot, 0:kk * nn].rearrange("p (k n) -> p k n", k=kk)

    def w_flush():
        while wstate["next_load"] < min(wstate["next_get"] + 4, len(wsched)):
            w_issue()

    pstate = {}
    ppools = {"g": [0, 1, 2, 3, 4, 5, 6], "n": [7], "x": [0, 1, 2, 3], "y": [4, 5, 6, 7], "bg": [5], "x6": [0, 1, 2, 3, 4, 5], "a1": [6]}

    def bank(pool="g"):
        lst = ppools[pool]
        k = pstate.get(pool, 0)
        pstate[pool] = k + 1
        return PSB[lst[k % len(lst)]]

    def vcol(c):
        return VEC[:, c:c + 1]

    EPSV = tr.sb("EPSV", [P, 1], F32)
    tr.op("dve", "memset", EPSV[:], EPS)
    VEPS = EPSV[:, 0:1]

    sq_state = {"n": 0}

    class RmsAcc:
        def __init__(self, nchunks, ncols, lhs_ones, pool):
            self.n = nchunks
            self.ncols = ncols
            self.lhs = lhs_ones
            self.acc = bank(pool)
            self.i = 0

        def feed(self, ch):
            sl = SQR[:, sq_state["n"] % 2, 0:self.ncols]
            sq_state["n"] += 1
            tr.op("act", "activation", sl, ch, AF.Square)
            tr.mm(self.acc[:, 0:self.ncols], self.lhs, sl, start=(self.i == 0), stop=(self.i == self.n - 1), inc=True)
            self.i += 1

        def finish(self, scale_div, dst_r):
            assert self.i == self.n
            tr.op("act", "activation", dst_r, self.acc[:, 0:self.ncols], AF.Ln, bias=VEPS, scale=1.0 / scale_div)
            tr.op("act", "activation", dst_r, dst_r, AF.Exp, scale=-0.5)

    def rms_sum(chunks, ncols, lhs_ones, scale_div, dst_r, pool="g"):
        acc = RmsAcc(len(chunks), ncols, lhs_ones, pool)
        for ch in chunks:
            acc.feed(ch)
        acc.finish(scale_div, dst_r)

    def xnorm_apply(src, gcol0, dst, ncols):
        chunks = src if isinstance(src, list) else [src[:, c, 0:ncols] for c in range(KC)]
        for c in range(KC):
            tr.op("dve", "scalar_tensor_tensor", dst[:, c, 0:ncols], chunks[c], vcol(gcol0 + c),
                  R_N[:, 0:ncols], op0=ALU.mult, op1=ALU.mult)

    def xnorm(src, gcol0, dst, ncols):
        chunks = src if isinstance(src, list) else [src[:, c, 0:ncols] for c in range(KC)]
        rms_sum(chunks, ncols, ONES, float(D), R_N[:, 0:ncols], pool="n")
        xnorm_apply(chunks, gcol0, dst, ncols)

    def proj_fm(wt, nchunks_out, rhs_of_k, ncols, nk=KC, col0=0, pool="g", flush=True):
        outs = []
        for oc in range(nchunks_out):
            b = bank(pool)
            for k in range(nk):
                tr.mm(b[:, 0:ncols], wt[:, k, col0 + oc * 128:col0 + (oc + 1) * 128], rhs_of_k(k),
                      start=(k == 0), stop=(k == nk - 1), inc=(k == nk - 1))
            outs.append(b[:, 0:ncols])
        if flush:
            w_flush()
        return outs

    if STOP_AFTER in ("B", "C"):
        xnorm(MTF, V_GMEM, HT, 256)
        for jj in range(4):
            wt = w_get(f"xkv{jj}")
            xs = proj_fm(wt, 2, lambda k: HT[:, k, 0:256], 256)
            rms_sum(xs, 256, ONES, 256.0, R2[:, 0:256])
            for dc in range(2):
                tr.op("dve", "scalar_tensor_tensor", XK[:, jj * 2 + dc, :], xs[dc], vcol(V_XKG + dc),
                      R2[:, 0:256], op0=ALU.mult, op1=ALU.mult)
        for jj in range(4):
            wt = w_get(f"xkv{4 + jj}")
            for mt in range(2):
                b = bank()
                for k in range(KC):
                    tr.mm(b[:, 0:256], HT[:, k, mt * 128:(mt + 1) * 128], wt[:, k, :], start=(k == 0), stop=(k == KC - 1),
                          inc=(k == KC - 1))
                tr.op("dve", "tensor_copy", XV[:, mt, jj * 256:(jj + 1) * 256], b[:, 0:256])

    ost_n = {"n": 0}
    for i in range(NT):
        t0 = i * T
        if i > 0 and STOP_AFTER == "C":
            xnorm([XT[:, c, :] for c in range(4)] + [XSTG[:, c, :] for c in range(4)], V_GMIX, HT, T)
            for c in range(4):
                tr.op("act", "activation", XT[:, 4 + c, :], XSTG[:, c, :], AF.Copy)
        else:
            xnorm(XT, V_GMIX, HT, T)

        def hrhs(k):
            return HT[:, k, :]

        def qk_a(xp, c, is_k, buf):
            gcol = vcol(V_KG if is_k else V_QG)
            sq = SQR[:, sq_state["n"] % 2, :]
            sq_state["n"] += 1
            xsb = XSB[:, buf, :]
            tr.op("act", "activation", sq, xp, AF.Square)
            tr.op("act", "activation", xsb, xp, AF.Copy, scale=gcol)
            ssb = bank("y")
            tr.mm(ssb[:, :], BONES, sq, start=True, stop=True, inc=True)
            rtb = bank("y")
            tr.mm(rtb[:, :], ROTM, xsb, start=True, stop=True, inc=True)
            return ssb, rtb

        def qk_b(xp, c, is_k, buf, ssb, rtb):
            gcol = vcol(V_KG if is_k else V_QG)
            R2_, T1, T2 = R2s[buf], T1s[buf], T2s[buf]
            tr.op("act", "activation", R2_, ssb[:, :], AF.Ln, bias=VEPS, scale=1.0 / 64.0)
            tr.op("act", "activation", R2_, R2_, AF.Exp, scale=-0.5)
            tr.op("dve", "scalar_tensor_tensor", T1, xp, gcol, COS[:], op0=ALU.mult, op1=ALU.mult)
            tr.op("dve", "tensor_tensor", T2, rtb[:, :], SIN[:], op=ALU.mult)
            tr.op("dve", "tensor_tensor", T1, T1, T2, op=ALU.add)
            for par in range(2):
                h = 2 * c + par
                if is_k:
                    d_ap = KT[0:64, h, t0:t0 + T]
                else:
                    d_ap = QTA[0:64, h, :]
                if is_k:
                    for bl in range(2):
                        tr.op("dve", "scalar_tensor_tensor", KT[0:64, h, t0 + 256 * bl:t0 + 256 * (bl + 1)],
                              T1[par * 64:(par + 1) * 64, 256 * bl:256 * (bl + 1)], 1.0,
                              R2_[par * 64:(par + 1) * 64, 256 * bl:256 * (bl + 1)], op0=ALU.mult, op1=ALU.mult,
                              accum_out=KMF[0:64, h, bl:bl + 1])
                else:
                    tr.op("dve", "tensor_tensor", d_ap, T1[par * 64:(par + 1) * 64, :],
                          R2_[par * 64:(par + 1) * 64, :], op=ALU.mult)

        chunks = []
        done_a = []
        nb = 0
        for is_k, tags in ((True, ("in4", "in5")), (False, ("in2", "in3"))):
            for jj, tag in enumerate(tags):
                wt = w_get(tag)
                xs = proj_fm(wt, 2, hrhs, T, pool="x")
                prev = done_a
                done_a = []
                for cc in range(2):
                    if cc < len(prev):
                        qk_b(*prev[cc])
                    ch = (xs[cc], jj * 2 + cc, is_k, nb % 2)
                    nb += 1
                    done_a.append(ch + qk_a(*ch))
        for ch in done_a:
            qk_b(*ch)

        tr.op("dve", "tensor_copy", KM[0:64, :, 2 * i:2 * i + 2], KMF[0:64, :, :])
        for s in range(4):
            gb = bank()
            for h in range(8):
                tr.mm(gb[:, h * 16:(h + 1) * 16], QTA[0:64, h, s * 128:(s + 1) * 128], KM[0:64, h, :],
                      start=True, stop=True, inc=(h == 7))
            tr.op("dve", "tensor_copy", GALL[:, s, :], gb[:, 0:128])

        for jj, tag in enumerate(("in6", "in7")):
            wt = w_get(tag)
            for s in range(4):
                b = bank()
                for k in range(KC):
                    tr.mm(b[:, 0:256], HT[:, k, s * 128:(s + 1) * 128], wt[:, k, :], start=(k == 0), stop=(k == KC - 1),
                          inc=(k == KC - 1))
                if s == 3:
                    w_flush()
                tr.op("act", "activation", VA[:, 4 * i + s, 4 * jj:4 * jj + 4, 0:64],
                      b[:, 0:256].rearrange("p (h d) -> p h d", h=4), AF.Copy)

        for s in range(4):
            qblk = 2 * i + s // 2
            BBv = BB[:, 64:192].rearrange("p (h n) -> p h n", h=8)
            if qblk == 0:
                tr.op("dve", "memset", BB[:, 64:192], 0.0)
            else:
                Gm = GALL[:, s, :].rearrange("p (h n) -> p h n", h=8)
                GW = G_.rearrange("p (h n) -> p h n", h=8)
                GE = PLf[:, 0:128].rearrange("p (h n) -> p h n", h=8)
                mx = M8[:, 0:8]
                mxb = mx.unsqueeze(2).to_broadcast([P, 8, 16])
                if qblk < 16:
                    tr.op("dve", "memset", Gm[:, :, qblk:16], -1e30)
                src_ = Gm
                for rnd in range(2):
                    tr.op("dve", "tensor_reduce", mx, src_, axis=AX.X, op=ALU.max)
                    tr.op("dve", "tensor_tensor", GE, src_, mxb, op=ALU.is_ge)
                    tr.op("dve", "scalar_tensor_tensor", GW, GE, -1e30, src_, op0=ALU.mult, op1=ALU.add)
                    src_ = GW
                tr.op("dve", "tensor_reduce", mx, GW, axis=AX.X, op=ALU.max)
                tr.op("dve", "tensor_tensor", GW, Gm, mxb, op=ALU.is_lt)
                tr.op("dve", "tensor_scalar", BBv, GW, NEG, None, op0=ALU.mult)
                if qblk < 16:
                    tr.op("dve", "memset", BBv[:, :, qblk:16], 0.0)
            for hh in range(2):
                pb = bank()
                for hl in range(4):
                    h = 4 * hh + hl
                    tr.mm(pb[0:80, hl * 128:(hl + 1) * 128], BB[:, 16 * h:16 * h + 80], IDENT, start=True, stop=True,
                          inc=(hl == 3))
                tr.op("act", "activation", QTA[64:80, 4 * hh:4 * hh + 4, s * 128:(s + 1) * 128],
                      pb[64:80, :].rearrange("p (h q) -> p h q", h=4), AF.Copy)

        if i + 1 < NT:
            tr.dma("sp", "cos", COS[:], cosT_d[:, t0 + T:t0 + 2 * T])
            tr.dma("sp", "sin", SIN[:], sinT_d[:, t0 + T:t0 + 2 * T])

        ustate = {}

        def u_task(g, pb_):
            if g % 2 == 0:
                ustate["wt"] = w_get(f"in{g // 2}")
            wt = ustate["wt"]
            c0 = (g % 2) * 128
            for k in range(KC):
                tr.mm(pb_, wt[:, k, c0:c0 + 128], HT[:, k, :], start=(k == 0), stop=(k == KC - 1), inc=(k == KC - 1))
            if g % 2 == 1:
                w_flush()
            w = 2 ** (g + 1)
            tr.op("dve", "tensor_copy", UB[:, 0:16], UH[:, g, :])
            tr.op("dve", "tensor_copy", UB[:, 16:528], pb_)
            tr.op("dve", "tensor_copy", UH[:, g, :], UB[:, 512:528])
            src = UB
            bufs = [SA, SBb]
            sh = 1
            nbb = 0
            lo = 0
            while sh < w:
                dstb = bufs[nbb % 2]
                lo += sh
                tr.op("dve", "tensor_tensor", dstb[:, lo:528], src[:, lo:528], src[:, lo - sh:528 - sh], op=ALU.add)
                src = dstb
                nbb += 1
                sh *= 2
            tr.op("dve", "scalar_tensor_tensor", PL, src[:, 16:528], 1.0 / w, UB[:, 16:528],
                  op0=ALU.mult, op1=ALU.subtract)
            if i == 0:
                tr.op("dve", "tensor_tensor", TMP16, src[:, 16:32], INVC[:, g * 16:(g + 1) * 16], op=ALU.mult)
                tr.op("dve", "tensor_tensor", PL[:, 0:16], TMP16, UB[:, 16:32], op=ALU.subtract)

        def u_task2(g, pb_):
            tr.mm(pb_, PW[:, g, :], PL, start=True, stop=True, inc=True)
            tr.op("dve", "tensor_scalar", MIX[:, g, :], pb_, vcol(V_PSC + g), None, op0=ALU.mult)

        SD = [DD[0], DD[1], DD[2]]
        OBK = [PSB[6], PSB[7]]
        PT3 = [PT2v[:, 0, :], PT2v[:, 1, :], T1s[0].bitcast(BF16)]
        nkt = 4 * i + 4
        units = []
        for h in range(8):
            for kt in range(0, 4 * i, 2):
                units.append(("pair", h, kt))
            for j in range(4):
                units.append(("single", h, 4 * i + j))
        nreal = len(units)
        for g in range(4):
            p1 = (g + 1) * nreal // 5 + 2 * g
            units.insert(p1, ("bg1", g, 0))
            units.insert(p1 + 4, ("bg2", g, 0))
        LA = 2
        deferred = []

        def qk(m):
            kind, h, kt = units[m]
            sd = SD[m % 3]
            if kind == "bg1":
                u_task(h, sd[:, 0:T])
            elif kind == "bg2":
                u_task2(h, sd[:, 0:T])
            elif kind == "pair":
                for e in range(2):
                    tr.mm(sd[:, e * T:(e + 1) * T], KT[0:80, h, (kt + e) * 128:(kt + e + 1) * 128], QTA[0:80, h, :],
                          start=True, stop=True, inc=(e == 1))
                tr.op("act", "activation", PT3[m % 3], sd[:, :], AF.Exp, scale=0.125)
            else:
                j = kt - 4 * i
                q0 = 128 * j
                tr.mm(sd[:, q0:T], KT[0:80, h, kt * 128:(kt + 1) * 128], QTA[0:80, h, q0:T], start=True, stop=False,
                      inc=False)
                tr.mm(sd[:, q0:q0 + 128], IDENT, TRI, start=False, stop=True, inc=True)
                tr.op("act", "activation", PT3[m % 3][:, q0:T], sd[:, q0:T], AF.Exp, scale=0.125)

        def pv(m):
            kind, h, kt = units[m]
            if kind in ("bg1", "bg2"):
                return
            ob = OBK[h % 2]
            if kind == "pair":
                for e in range(2):
                    tr.mm(ob[0:65, :], VA[:, kt + e, h, :], PT3[m % 3][:, e * T:(e + 1) * T], start=(kt + e == 0),
                          stop=False, inc=(e == 1))
            else:
                j = kt - 4 * i
                q0 = 128 * j
                last = (kt == nkt - 1)
                tr.mm(ob[0:65, q0:T], VA[:, kt, h, :], PT3[m % 3][:, q0:T], start=(kt == 0), stop=last, inc=True)
                if last:
                    if h == 7:
                        tr.op("act", "activation", RD[64:65, :], ob[64:65, :], AF.Ln)
                        tr.op("act", "activation", RD[64:65, :], RD[64:65, :], AF.Exp, scale=-1.0)
                    else:
                        tr.op("dve", "reciprocal", RD[64:65, :], ob[64:65, :])

                    def fin(h=h, ob=ob):
                        rb = ob[64:128, :]
                        tr.mm(rb, ONE32[64:65, 0:64], RD[64:65, :], start=True, stop=True, inc=True)
                        tr.op("dve", "tensor_copy", RBS[0:64, :], rb)
                        par = h % 2
                        tr.op("dve", "tensor_tensor", MIX[par * 64:(par + 1) * 64, 4 + h // 2, :], ob[0:64, :],
                              RBS[0:64, :], op=ALU.mult)
                    deferred.append((m + 4, fin))

        for m in range(len(units) + LA):
            if m < len(units):
                qk(m)
            if m - LA >= 0:
                pv(m - LA)
            while deferred and deferred[0][0] <= m - LA:
                deferred.pop(0)[1]()
        while deferred:
            deferred.pop(0)[1]()

        if i == 0:
            dbg("mix", MIX)
            dbg("qta", QTA)
            dbg("kt", KT[0:80, :, 0:512])
        nacc = RmsAcc(KC, T, ONES, "n") if STOP_AFTER in ("B", "C") else None
        pend_feed = []
        for jj in range(4):
            wt = w_get(f"out{jj}")
            xs = proj_fm(wt, 2, lambda k: MIX[:, k, :], T)
            for oc_ in pend_feed:
                nacc.feed(XT[:, oc_, :])
            pend_feed = []
            for cc in range(2):
                oc = jj * 2 + cc
                tr.op("dve", "tensor_tensor", XT[:, oc, :], xs[cc], XT[:, oc, :], op=ALU.add)
                if nacc is not None:
                    pend_feed.append(oc)
        for oc_ in pend_feed:
            nacc.feed(XT[:, oc_, :])

        if STOP_AFTER in ("B", "C"):
            nacc.finish(float(D), R_N)
            xnorm_apply(XT, V_GX, HT, T)
            def finish_xq(hd, xs):
                rms_sum(xs, T, ONES, 256.0, R2s[hd % 2], pool="a1")
                for dc in range(2):
                    tr.op("dve", "scalar_tensor_tensor", XQ[:, hd * 2 + dc, :], xs[dc], vcol(V_XQG + dc), R2s[hd % 2],
                          op0=ALU.mult, op1=ALU.mult)

            prev = None
            for hd in range(4):
                wt = w_get(f"xq{hd}")
                xs = proj_fm(wt, 2, hrhs, T, pool="x6")
                if prev is not None:
                    finish_xq(*prev)
                prev = (hd, xs)
            finish_xq(*prev)

            def xsc(hd):
                pslots = []
                for mt in range(2):
                    sb_ = bank("x")
                    for dc in range(2):
                        tr.mm(sb_[:, :], XK[:, hd * 2 + dc, mt * 128:(mt + 1) * 128], XQ[:, hd * 2 + dc, :],
                              start=(dc == 0), stop=(dc == 1), inc=(dc == 1))
                    slot = (hd * 2 + mt) % 4
                    tr.op("act", "activation", PT[:, slot, :], sb_[:, :], AF.Exp, scale=1.0 / 16.0)
                    pslots.append(slot)
                return pslots

            def xpv(hd, pslots):
                db = bank("y")
                for mt in range(2):
                    tr.mm(db[:, :], ONES, PT[:, pslots[mt], :], start=(mt == 0), stop=(mt == 1), inc=(mt == 1))
                tr.op("act", "activation", RDEN, db[:, :], AF.Ln)
                tr.op("act", "activation", RDEN, RDEN, AF.Exp, scale=-1.0)
                for dch in range(2):
                    ob = bank("y")
                    for mt in range(2):
                        tr.mm(ob[:, :], XV[:, mt, (hd * 2 + dch) * 128:(hd * 2 + dch + 1) * 128], PT[:, pslots[mt], :],
                              start=(mt == 0), stop=(mt == 1), inc=(mt == 1))
                    tr.op("dve", "tensor_tensor", XO[:, hd * 2 + dch, :], ob[:, :], RDEN, op=ALU.mult)

            prev = None
            for hd in range(4):
                ps_ = xsc(hd)
                if prev is not None:
                    xpv(*prev)
                prev = (hd, ps_)
            xpv(*prev)
            nacc = RmsAcc(KC, T, ONES, "n") if STOP_AFTER == "C" else None
            pend_feed = []
            for jj in range(4):
                wt = w_get(f"xo{jj}")
                xs = proj_fm(wt, 2, lambda k: XO[:, k, :], T)
                for oc_ in pend_feed:
                    nacc.feed(XT[:, oc_, :])
                pend_feed = []
                for cc in range(2):
                    oc = jj * 2 + cc
                    tr.op("dve", "tensor_tensor", XT[:, oc, :], xs[cc], XT[:, oc, :], op=ALU.add)
                    if nacc is not None:
                        pend_feed.append(oc)
            for oc_ in pend_feed:
                nacc.feed(XT[:, oc_, :])

        if STOP_AFTER == "C":
            nacc.finish(float(D), R_N)
            xnorm_apply(XT, V_GFFN, HT, T)
            ffn_n = {"n": 0}

            ZR = ZZ[:, i % 2]
            ZW = ZZ[:, (i + 1) % 2]

            def pre_chunk(pp, ch):
                r = ffn_n["n"] % 4
                ffn_n["n"] += 1
                A = AB[r]
                w0, w1, w2 = (vcol(V_CW + ch * 3 + jx) for jx in range(3))
                tr.op("act", "activation", A, pp, AF.Identity, bias=vcol(V_CB + ch), scale=w2)
                tr.op("act", "activation", ZR[:, ch, 2:4], pp[:, 0:2], AF.Copy)
                tr.op("act", "activation", ZW[:, ch, 0:2], pp[:, T - 2:T], AF.Copy)
                tr.op("dve", "scalar_tensor_tensor", A[:, 2:T], pp[:, 1:T - 1], w1, A[:, 2:T], op0=ALU.mult, op1=ALU.add)
                tr.op("dve", "scalar_tensor_tensor", A[:, 2:T], pp[:, 0:T - 2], w0, A[:, 2:T], op0=ALU.mult, op1=ALU.add)
                return A

            def up_pair(j, slot0):
                wg = w_get(f"upg{j}")
                gs = proj_fm(wg, 2, hrhs, T)
                wv = w_get(f"upv{j}")
                vs = proj_fm(wv, 2, hrhs, T)
                for cc in range(2):
                    ch = 2 * j + cc
                    ag = pre_chunk(gs[cc], ch)
                    av = pre_chunk(vs[cc], 22 + ch)
                    tr.op("act", "activation", ag, ag, AF.Silu)
                    tr.op("pool", "tensor_tensor", ACTT[:, slot0 + cc, :], ag, av, op=ALU.mult)

            def fix_cols01(c0, c1, slot0):
                n = c1 - c0
                ps_ = []
                for base in (0, 22):
                    lo, hi = base + c0, base + c1
                    pa = (FXA if base == 0 else FXB)[:, 0:n, :]
                    tmp = FXT[:, 0:n, :]
                    tr.op("dve", "tensor_tensor", pa, ZR[:, lo:hi, 2:4], WB[2][:, lo:hi, :], op=ALU.mult)
                    tr.op("dve", "tensor_tensor", tmp, ZR[:, lo:hi, 1:3], WB[1][:, lo:hi, :], op=ALU.mult)
                    tr.op("dve", "tensor_tensor", pa, pa, tmp, op=ALU.add)
                    tr.op("dve", "tensor_tensor", tmp, ZR[:, lo:hi, 0:2], WB[0][:, lo:hi, :], op=ALU.mult)
                    tr.op("dve", "tensor_tensor", pa, pa, tmp, op=ALU.add)
                    tr.op("dve", "tensor_tensor", pa, pa, WB[3][:, lo:hi, :], op=ALU.add)
                    ps_.append(pa)
                tr.op("act", "activation", ps_[0], ps_[0], AF.Silu)
                tr.op("dve", "tensor_tensor", ACTT[:, slot0:slot0 + n, 0:2], ps_[0], ps_[1], op=ALU.mult)

            def down(hf, slots):
                nk = len(slots)
                for oc in range(8):
                    wt = w_get(f"dn{hf}_{oc}")
                    b = bank()
                    for kk in range(nk):
                        tr.mm(b[:, :], wt[:, kk, :], ACTT[:, slots[kk], :], start=(kk == 0), stop=(kk == nk - 1),
                              inc=(kk == nk - 1))
                    w_flush()
                    if hf == 0:
                        tr.op("dve", "tensor_tensor", XT[:, oc, :], b[:, :], XT[:, oc, :], op=ALU.add)
                    else:
                        emit_out(oc, b)

            def emit_out(oc, b):
                sl = ost_n["n"] % 2
                ost_n["n"] += 1
                tr.op("dve", "tensor_tensor", OST[:, sl, :], b[:, :], XT[:, oc, :], op=ALU.add)
                tr.dma("sp", f"ost{sl}", yT[:, oc, t0:t0 + T], OST[:, sl, :])
                if i + 1 < NT and oc < 4:
                    tr.dma("sp", f"x{oc}", XT[:, oc, :], xT[:, oc, t0 + T:t0 + 2 * T])

            for j in range(6):
                up_pair(j, 2 * j)
            fix_cols01(0, 12, 0)
            for j in range(6, 8):
                up_pair(j, 12 + 2 * (j - 6))
            fix_cols01(12, 16, 12)
            down(0, list(range(12)))
            for j in range(8, 11):
                up_pair(j, 2 * (j - 8))
            fix_cols01(16, 22, 0)
            if i + 1 < NT:
                for c in range(4):
                    tr.dma("sp", f"x{4 + c}", XSTG[:, c, :], xT[:, 4 + c, t0 + T:t0 + 2 * T])
            down(1, [12, 13, 14, 15, 0, 1, 2, 3, 4, 5])
        else:
            for oc in range(8):
                sl = ost_n["n"] % 2
                ost_n["n"] += 1
                tr.op("dve", "tensor_copy", OST[:, sl, :], XT[:, oc, :])
                tr.dma("sp", f"ost{sl}", yT[:, oc, t0:t0 + T], OST[:, sl, :])
                if i + 1 < NT:
                    tr.dma("sp", f"x{oc}", XT[:, oc, :], xT[:, oc, t0 + T:t0 + 2 * T])

    for k in ("ost0", "ost1"):
        nc.sync.wait_ge(tr.sem[k], tr.cnt[k])
    for k in dbg_outs:
        nc.gpsimd.wait_ge(tr.sem[k], tr.cnt[k])
    es.close()
    return nc, tr


def _host_consts():
    half = 32
    inv_freq = (np.float32(10000.0) ** (-np.arange(half, dtype=np.float32) / np.float32(half))).astype(np.float32)
    pos = np.arange(S, dtype=np.float32)
    ang = (pos[:, None] * inv_freq[None, :]).astype(np.float32)
    cos = np.cos(ang).astype(np.float32)
    sin = np.sin(ang).astype(np.float32)
    idx = (np.arange(P) % 64) % 32
    cosT = np.ascontiguousarray(cos[:, idx].T)
    sinT = np.ascontiguousarray(sin[:, idx].T)
    cst = np.zeros((P, NCB), np.float32)
    cst[:, C_ID:C_ID + 128] = np.eye(P, dtype=np.float32)
    cst[:, C_ONES:C_ONES + 128] = 1.0
    for b in range(2):
        cst[64 * b:64 * b + 64, C_BONES + 64 * b:C_BONES + 64 * b + 64] = 1.0
    kk = np.arange(P)[:, None]
    qq = np.arange(P)[None, :]
    cst[:, C_TRI:C_TRI + 128] = np.where(kk <= qq, 0.0, NEG)
    rot = np.zeros((P, P), np.float32)
    for d in range(P):
        if (d % 64) < 32:
            rot[d + 32, d] = -1.0
        else:
            rot[d - 32, d] = 1.0
    cst[:, C_ROT:C_ROT + 128] = rot
    oneh = np.zeros((16, S), np.float32)
    for n in range(16):
        oneh[n, 256 * n:256 * n + 256] = 1.0
    invc = np.zeros((P, 64), np.float32)
    for g in range(4):
        w = 2 ** (g + 1)
        invc[:, g * 16:(g + 1) * 16] = (1.0 / np.minimum(np.arange(16) + 1, w)).astype(np.float32)[None, :]
    return cosT, sinT, cst, oneh, invc


_CACHE = {}


def kernel(x, mem, norm_mix_g, w_in, pool_w, pool_scale, q_norm_g, k_norm_g, w_out,
           norm_xattn_g, norm_mem_g, w_xq, w_xkv, xq_norm_g, xk_norm_g, w_xo,
           norm_ffn_g, w_up, conv_w, conv_b, w_down):
    f = lambda a: np.ascontiguousarray(np.asarray(a, dtype=np.float32))
    x = f(x)
    mem = f(mem)
    B = x.shape[0]
    vecs = np.zeros((P, NV), np.float32)
    vecs[:, V_GMIX:V_GMIX + 8] = f(norm_mix_g)[0].reshape(8, P).T
    vecs[:, V_GX:V_GX + 8] = f(norm_xattn_g)[0].reshape(8, P).T
    vecs[:, V_GMEM:V_GMEM + 8] = f(norm_mem_g)[0].reshape(8, P).T
    vecs[:, V_GFFN:V_GFFN + 8] = f(norm_ffn_g)[0].reshape(8, P).T
    vecs[:, V_PSC:V_PSC + 4] = f(pool_scale)[0].reshape(4, P).T
    vecs[:, V_QG] = np.tile(f(q_norm_g)[0], 2)
    vecs[:, V_KG] = np.tile(f(k_norm_g)[0], 2)
    vecs[:, V_XQG:V_XQG + 2] = f(xq_norm_g)[0].reshape(2, P).T
    vecs[:, V_XKG:V_XKG + 2] = f(xk_norm_g)[0].reshape(2, P).T
    cw = f(conv_w)[0]
    vecs[:, V_CW:V_CW + 132] = cw.reshape(3, 44, P).transpose(2, 1, 0).reshape(P, 132)
    vecs[:, V_CB:V_CB + 44] = f(conv_b)[0].reshape(44, P).T
    cosT, sinT, cst, oneh, invc = _host_consts()
    wdn = f(w_down)[0]
    wd0 = np.ascontiguousarray(wdn[0:1536].reshape(12, P, 8, 128).transpose(2, 1, 0, 3).reshape(8, P, 12 * 128))
    wd1 = np.ascontiguousarray(wdn[1536:2816].reshape(10, P, 8, 128).transpose(2, 1, 0, 3).reshape(8, P, 10 * 128))
    if "nc" not in _CACHE:
        _CACHE["nc"] = build_program()[0]
    nc = _CACHE["nc"]
    shared = {
        "w_in": f(w_in)[0], "w_out": f(w_out)[0], "w_xq": f(w_xq)[0], "w_xkv": f(w_xkv)[0], "w_xo": f(w_xo)[0],
        "w_up": f(w_up)[0], "wd0": wd0, "wd1": wd1, "pool_w": f(pool_w)[0], "vecs": vecs, "cosT": cosT, "sinT": sinT,
        "cst": cst, "oneh": oneh, "invc": invc,
    }
    in_maps = []
    for b in range(B):
        m = dict(shared)
        m["xT"] = np.ascontiguousarray(x[b].T)
        m["memT"] = np.ascontiguousarray(mem[b].T)
        in_maps.append(m)
    res = run_bass_kernel_spmd(nc, in_maps, core_ids=list(range(B)))
    out = np.stack([np.ascontiguousarray(res.results[b]["yT"].T) for b in range(B)], axis=0)
    return out.astype(np.float32)
```
